# Optimizing a Trainium2 kernel written in Bass

```python
import math
import jax, jax.numpy as jnp
from jax import lax
import numpy as np

D_MODEL = 1024
BATCH = 4
SEQ = 8192
DEPTH = 2
DEC_BATCH = 8
DEC_SEQ = 64
PAST_LEN = 1024

CHUNK = 64
N_A = DEPTH // 2
N_B = DEPTH - N_A
RMS_EPS = 1e-5
RW_HEAD = 64
RW_HEADS = D_MODEL // RW_HEAD
D_DECAY_LORA = 64
D_AAA_LORA = 64
D_GATE_LORA = 160
LNX_EPS = 64e-5
HEAD_DIM = 64
N_Q_HEADS = D_MODEL // HEAD_DIM
N_KV_HEADS = 4
GROUP = N_Q_HEADS // N_KV_HEADS
WINDOW = 128
WIN_CHUNKS = WINDOW // CHUNK
ROT_DIM = HEAD_DIM // 4
ROPE_THETA = 500000.0
ATTN_SCALE = 1.0 / math.sqrt(HEAD_DIM)
NEG_INF = -1e30
D_FF = ((8 * D_MODEL // 3 + 255) // 256) * 256

kernel_name = "yoco_rwkv7_swa_sink_stream_step"


def rmsnorm(x, g):
    xf = x.astype(jnp.float32)
    y = xf * lax.rsqrt(jnp.mean(xf * xf, axis=-1, keepdims=True) + RMS_EPS)
    return (y * g.astype(jnp.float32)).astype(x.dtype)


def swiglu(x, w_in, w_out):
    gu = x @ w_in
    return (jax.nn.silu(gu[..., :D_FF]) * gu[..., D_FF:]) @ w_out


def partial_rope(x, pos):
    half = ROT_DIM // 2
    inv = jnp.power(jnp.float32(ROPE_THETA), -jnp.arange(half, dtype=jnp.float32) * (2.0 / ROT_DIM))
    ang = pos.astype(jnp.float32)[:, None] * inv[None, :]
    cos = jnp.cos(ang)[None, :, None, :]
    sin = jnp.sin(ang)[None, :, None, :]
    xr = x[..., :ROT_DIM].astype(jnp.float32)
    x1, x2 = xr[..., :half], xr[..., half:]
    rot = jnp.concatenate([x1 * cos - x2 * sin, x2 * cos + x1 * sin], axis=-1).astype(x.dtype)
    return jnp.concatenate([rot, x[..., ROT_DIM:]], axis=-1)


def rwkv7_time_mix(x, shift_prev, s0, mu, w_rkv, w0, w1, w2, a0, a1, a2, g1, g2,
                   k_k, k_a, r_k, lnx_w, lnx_b, w_o):
    f32 = jnp.float32
    bsz, t_len, _ = x.shape
    x_prev = jnp.concatenate([shift_prev[:, None, :].astype(x.dtype), x[:, :-1]], axis=1)
    xx = x_prev - x
    xr = x + xx * mu[0]
    xw = x + xx * mu[1]
    xk = x + xx * mu[2]
    xv = x + xx * mu[3]
    xa = x + xx * mu[4]
    xg = x + xx * mu[5]
    r = xr @ w_rkv[0]
    k = xk @ w_rkv[1]
    v = xv @ w_rkv[2]
    w_log = -jax.nn.softplus(-(w0 + jnp.tanh(xw @ w1) @ w2)) - 0.5
    decay = jnp.exp(-jnp.exp(w_log.astype(f32)))
    a = jax.nn.sigmoid(a0 + (xa @ a1) @ a2)
    g = jax.nn.sigmoid(xg @ g1) @ g2

    def heads(t):
        return t.astype(f32).reshape(bsz, t_len, RW_HEADS, RW_HEAD)

    kk = heads(k * k_k)
    kk = kk * lax.rsqrt(jnp.maximum(jnp.sum(kk * kk, axis=-1, keepdims=True), 1e-24))
    k = k * (1 + (a - 1) * k_a)
    rh, kh, vh, wh, ah = heads(r), heads(k), heads(v), heads(decay), heads(a)
    a_vec = -kk
    b_vec = kk * ah

    def step(s, inp):
        r_t, w_t, k_t, v_t, a_t, b_t = inp
        sa = jnp.einsum('bhvk,bhk->bhv', s, a_t)
        s = s * w_t[:, :, None, :] + sa[..., None] * b_t[:, :, None, :] + v_t[..., None] * k_t[:, :, None, :]
        return s, jnp.einsum('bhvk,bhk->bhv', s, r_t)

    seq_in = tuple(jnp.swapaxes(t, 0, 1) for t in (rh, wh, kh, vh, a_vec, b_vec))
    s_fin, y = lax.scan(step, s0.astype(f32), seq_in)
    y = jnp.swapaxes(y, 0, 1)
    yc = y - jnp.mean(y, axis=-1, keepdims=True)
    yn = yc * lax.rsqrt(jnp.mean(yc * yc, axis=-1, keepdims=True) + LNX_EPS)
    yn = yn.reshape(bsz, t_len, D_MODEL) * lnx_w.astype(f32) + lnx_b.astype(f32)
    bonus = (jnp.sum(rh * kh * r_k.astype(f32), axis=-1, keepdims=True) * vh).reshape(bsz, t_len, D_MODEL)
    out = ((yn + bonus) * g.astype(f32)).astype(x.dtype) @ w_o
    return out, x[:, -1], s_fin.astype(s0.dtype)


def shared_kv(h, kv_norm, w_kv, b_kv, pos):
    bsz, t_len, _ = h.shape
    kv = rmsnorm(h, kv_norm) @ w_kv + b_kv
    k = kv[..., :N_KV_HEADS * HEAD_DIM].reshape(bsz, t_len, N_KV_HEADS, HEAD_DIM)
    v = kv[..., N_KV_HEADS * HEAD_DIM:].reshape(bsz, t_len, N_KV_HEADS, HEAD_DIM)
    return partial_rope(k, pos), v


def sink_softmax(scores, sinks):
    sk = sinks.astype(jnp.float32)[:, :, None, None]
    m = jnp.maximum(jnp.max(scores, axis=-1, keepdims=True), sk)
    p = jnp.exp(scores - m)
    return p / (jnp.sum(p, axis=-1, keepdims=True) + jnp.exp(sk - m))


def swa_band_attention(q, k, v, sinks):
    bsz, t_len = q.shape[:2]
    n_chunks = t_len // CHUNK
    qc = q.reshape(bsz, n_chunks, CHUNK, N_KV_HEADS, GROUP, HEAD_DIM)

    def band(t):
        tc = t.reshape(bsz, n_chunks, CHUNK, N_KV_HEADS, HEAD_DIM)
        tp = jnp.pad(tc, ((0, 0), (WIN_CHUNKS, 0), (0, 0), (0, 0), (0, 0)))
        return jnp.concatenate([tp[:, j:j + n_chunks] for j in range(WIN_CHUNKS + 1)], axis=2)

    kb, vb = band(k), band(v)
    s = jnp.einsum('bnqhgd,bnkhd->bnhgqk', qc, kb, preferred_element_type=jnp.float32) * ATTN_SCALE
    key_chunk = (jnp.arange(n_chunks)[:, None] - WIN_CHUNKS
                 + jnp.arange((WIN_CHUNKS + 1) * CHUNK)[None, :] // CHUNK)
    valid = key_chunk >= 0
    s = jnp.where(valid[None, :, None, None, None, :], s, NEG_INF)
    p = sink_softmax(s, sinks)
    o = jnp.einsum('bnhgqk,bnkhd->bnqhgd', p.astype(v.dtype), vb)
    return o.reshape(bsz, t_len, N_Q_HEADS * HEAD_DIM)


def swa_cached_attention(q, k_all, v_all, sinks):
    bsz, t_len = q.shape[:2]
    s = jnp.einsum('bqhgd,bkhd->bhgqk', q, k_all, preferred_element_type=jnp.float32) * ATTN_SCALE
    p = sink_softmax(s, sinks)
    o = jnp.einsum('bhgqk,bkhd->bqhgd', p.astype(v_all.dtype), v_all)
    return o.reshape(bsz, t_len, N_Q_HEADS * HEAD_DIM)


def setup_inputs(seed: int = 0) -> dict:
    key = jax.random.key(seed)
    ks = iter(jax.random.split(key, 48))
    nrm = lambda shape, scale: jax.random.normal(next(ks), shape, jnp.float32) * scale
    gain = lambda shape: 1.0 + nrm(shape, 0.02)
    D = D_MODEL
    KVW = N_KV_HEADS * HEAD_DIM
    QW = N_Q_HEADS * HEAD_DIM
    return {
        "x_prompt": nrm((BATCH, SEQ, D), 1.0),
        "x_sample": nrm((DEC_BATCH, DEC_SEQ, D), 1.0),
        "state_wkv": nrm((N_A, DEC_BATCH, RW_HEADS, RW_HEAD, RW_HEAD), 0.5),
        "state_shift": nrm((N_A, DEC_BATCH, D), 1.0),
        "cache_k": nrm((DEC_BATCH, WINDOW, N_KV_HEADS, HEAD_DIM), 1.0),
        "cache_v": nrm((DEC_BATCH, WINDOW, N_KV_HEADS, HEAD_DIM), 1.0),
        "norm_mix": gain((DEPTH, D)),
        "norm_ffn": gain((DEPTH, D)),
        "rw_mu": jax.random.uniform(next(ks), (N_A, 6, D), jnp.float32),
        "rw_w_rkv": nrm((N_A, 3, D, D), D ** -0.5),
        "rw_w0": -1.5 + nrm((N_A, D), 1.0),
        "rw_w1": nrm((N_A, D, D_DECAY_LORA), D ** -0.5),
        "rw_w2": nrm((N_A, D_DECAY_LORA, D), 0.1 * D_DECAY_LORA ** -0.5),
        "rw_a0": nrm((N_A, D), 0.1),
        "rw_a1": nrm((N_A, D, D_AAA_LORA), D ** -0.5),
        "rw_a2": nrm((N_A, D_AAA_LORA, D), 0.5 * D_AAA_LORA ** -0.5),
        "rw_g1": nrm((N_A, D, D_GATE_LORA), D ** -0.5),
        "rw_g2": nrm((N_A, D_GATE_LORA, D), D_GATE_LORA ** -0.5),
        "rw_k_k": 0.85 + nrm((N_A, D), 0.02),
        "rw_k_a": gain((N_A, D)),
        "rw_r_k": nrm((N_A, RW_HEADS, RW_HEAD), 0.1),
        "rw_lnx_w": gain((N_A, D)),
        "rw_lnx_b": nrm((N_A, D), 0.02),
        "rw_w_o": nrm((N_A, D, D), D ** -0.5),
        "kv_norm": gain((D,)),
        "w_kv": nrm((D, 2 * KVW), D ** -0.5),
        "b_kv": nrm((2 * KVW,), 0.02),
        "w_q": nrm((N_B, D, QW), D ** -0.5),
        "b_q": nrm((N_B, QW), 0.02),
        "attn_sinks": nrm((N_B, N_KV_HEADS, GROUP), 1.0),
        "w_o": nrm((N_B, QW, D), QW ** -0.5),
        "b_o": nrm((N_B, D), 0.02),
        "ffn_w_in": nrm((DEPTH, D, 2 * D_FF), D ** -0.5),
        "ffn_w_out": nrm((DEPTH, D_FF, D), D_FF ** -0.5),
        "norm_final": gain((D,)),
    }


def reference(x_prompt, x_sample, state_wkv, state_shift, cache_k, cache_v,
              norm_mix, norm_ffn, rw_mu, rw_w_rkv, rw_w0, rw_w1, rw_w2, rw_a0, rw_a1, rw_a2,
              rw_g1, rw_g2, rw_k_k, rw_k_a, rw_r_k, rw_lnx_w, rw_lnx_b, rw_w_o,
              kv_norm, w_kv, b_kv, w_q, b_q, attn_sinks, w_o, b_o, ffn_w_in, ffn_w_out, norm_final):

    def run(x, pos, shift0, s0, win_k, win_v):
        bsz, t_len, _ = x.shape
        shifts, states = [], []
        k_all = v_all = k_state = v_state = None
        for l in range(DEPTH):
            h = rmsnorm(x, norm_mix[l])
            if l < N_A:
                h, sh, st = rwkv7_time_mix(h, shift0[l], s0[l], rw_mu[l], rw_w_rkv[l], rw_w0[l], rw_w1[l],
                                           rw_w2[l], rw_a0[l], rw_a1[l], rw_a2[l], rw_g1[l], rw_g2[l],
                                           rw_k_k[l], rw_k_a[l], rw_r_k[l], rw_lnx_w[l], rw_lnx_b[l],
                                           rw_w_o[l])
                shifts.append(sh)
                states.append(st)
            else:
                if l == N_A:
                    k_new, v_new = shared_kv(x, kv_norm, w_kv, b_kv, pos)
                    if win_k is None:
                        k_all, v_all = k_new, v_new
                    else:
                        k_all = jnp.concatenate([win_k.astype(k_new.dtype), k_new], axis=1)
                        v_all = jnp.concatenate([win_v.astype(v_new.dtype), v_new], axis=1)
                    k_state, v_state = k_all[:, -WINDOW:], v_all[:, -WINDOW:]
                j = l - N_A
                q = (h @ w_q[j] + b_q[j]).reshape(bsz, t_len, N_Q_HEADS, HEAD_DIM)
                q = partial_rope(q, pos).reshape(bsz, t_len, N_KV_HEADS, GROUP, HEAD_DIM)
                if win_k is None:
                    o = swa_band_attention(q, k_all, v_all, attn_sinks[j])
                else:
                    o = swa_cached_attention(q, k_all, v_all, attn_sinks[j])
                h = o @ w_o[j] + b_o[j]
            x = x + h
            x = x + swiglu(rmsnorm(x, norm_ffn[l]), ffn_w_in[l], ffn_w_out[l])
        return rmsnorm(x, norm_final), jnp.stack(states), jnp.stack(shifts), k_state, v_state

    bp = x_prompt.shape[0]
    pos_p = jnp.arange(x_prompt.shape[1], dtype=jnp.int32)
    pos_s = PAST_LEN + jnp.arange(x_sample.shape[1], dtype=jnp.int32)
    shift_zero = jnp.zeros((N_A, bp, D_MODEL), x_prompt.dtype)
    wkv_zero = jnp.zeros((N_A, bp, RW_HEADS, RW_HEAD, RW_HEAD), x_prompt.dtype)
    y_prompt, wkv_p, shift_p, k_p, v_p = run(x_prompt, pos_p, shift_zero, wkv_zero, None, None)
    y_sample, wkv_s, shift_s, k_s, v_s = run(x_sample, pos_s, state_shift, state_wkv, cache_k, cache_v)
    return (y_prompt, y_sample, wkv_p, shift_p, k_p, v_p, wkv_s, shift_s, k_s, v_s)
```

```python
import contextlib
import numpy as np
import concourse.bass as bass
import concourse.mybir as mybir
from concourse.bass_utils import run_bass_kernel_spmd

F32 = mybir.dt.float32
BF16 = mybir.dt.bfloat16
AF = mybir.ActivationFunctionType
ALU = mybir.AluOpType
AX = mybir.AxisListType

D = 1024
NT = 256
DFF = 2816
NJ = DFF // 128
C0 = 0.6065306597126334
RMS_EPS = 1e-5
LNX_EPS = 64e-5
ATT_SCALE = 0.125
NEG = -1e30

(V_NM0, V_NF0, V_NM1, V_NF1, V_KVN, V_NFIN, V_MU0, V_MU1, V_MU2, V_MU3, V_MU4, V_MU5, V_BO,
 V_W0, V_A0, V_KK, V_KA, V_C1, V_RK, V_LNW, V_LNB) = range(21)
NVEC = 21

CS_ID = 0
CS_BLK = 128
CS_MS = 256
CS_MI = 320
CS_MST = 384
CS_I64 = 448
CS_MSBD = 512
CS_MIBD = 640
CS_MSTBD = 768
CS_SELBD = 896
CS_P16 = 1024
CS_RESET = 1040
CS_W = CS_RESET + NT


class Buf:
    __slots__ = ("name", "w", "r", "psum")

    def __init__(self, name):
        self.name = name
        self.w = None
        self.r = {}
        self.psum = False


class Prod:
    def __init__(self, name, semh, is_dma=False):
        self.name = name
        self.semh = semh
        self.is_dma = is_dma
        self.n = 0
        self.recs = []
        self.kcs = []


class Eng:
    def __init__(self, name, prod):
        self.name = name
        self.prod = prod
        self.seen = {}
        self.ops = []


class Sched:
    def _waits(self, E, reads, writes, relax=False):
        need = {}

        def req(ev):
            if ev is None:
                return
            p, v = ev
            if need.get(p, 0) < v:
                need[p] = v

        for b in reads:
            req(b.w)
            if b.psum:
                for p, v in b.r.items():
                    if p is not E.prod:
                        req((p, v))
        for b in writes:
            if b.w is not None and not (relax and b.w[0] is E.prod):
                req(b.w)
            for p, v in b.r.items():
                if not (relax and p is E.prod):
                    req((p, v))
        for p, v in need.items():
            if p is E.prod and E.name == "pe":
                continue
            if E.seen.get(p, 0) >= v:
                continue
            E.seen[p] = v
            kc = p.kcs[v - 1]
            if kc:
                for q, w in kc.items():
                    if E.seen.get(q, 0) < w:
                        E.seen[q] = w
            if not p.is_dma:
                p.recs[v - 1]["inc"] = True
            E.ops.append({"k": "w", "p": p, "v": v})

    @staticmethod
    def _mark(ev, reads, writes):
        p, v = ev
        for b in reads:
            if b.r.get(p, 0) < v:
                b.r[p] = v
        for b in writes:
            b.w = ev
            b.r = {}

    def op(self, E, fn, R=(), W=(), relax=False):
        self._waits(E, R, W, relax)
        p = E.prod
        p.n += 1
        rec = {"k": "o", "fn": fn, "inc": False, "p": p}
        p.recs.append(rec)
        p.kcs.append(dict(E.seen))
        E.ops.append(rec)
        self._mark((p, p.n), R, W)
        return rec

    def dma(self, Q, fn, dsem, R=(), W=()):
        self._waits(Q, R, W)
        dsem.n += 1
        dsem.kcs.append(dict(Q.seen))
        rec = {"k": "d", "fn": fn, "p": dsem}
        Q.ops.append(rec)
        self._mark((dsem, dsem.n), R, W)

    @staticmethod
    def finalize(prods):
        for p in prods:
            if p.is_dma:
                continue
            c = 0
            p.vals = []
            for r in p.recs:
                if r["inc"]:
                    c += 1
                p.vals.append(c)

    @staticmethod
    def replay(e, E):
        for o in E.ops:
            if o["k"] == "w":
                p = o["p"]
                val = 16 * o["v"] if p.is_dma else p.vals[o["v"] - 1]
                e.wait_ge(p.semh, val)
            elif o["k"] == "d":
                o["fn"](e).then_inc(o["p"].semh, 16)
            else:
                ins = o["fn"](e)
                if o["inc"]:
                    ins.then_inc(o["p"].semh, 1)


class TB:
    def __init__(self, t, name):
        self.t = t
        self.b = Buf(name)

    def __getitem__(self, idx):
        return self.t[idx]


def bc(ap, shape):
    return ap.broadcast_to(list(shape))


def build(n_warm, n_main, dbg=False):
    nc = bass.Bass("TRN2", target_bir_lowering=False)
    S = Sched()
    es = contextlib.ExitStack()

    def din(name, shape, dt=F32):
        return nc.dram_tensor(name, list(shape), dt, kind="ExternalInput").ap()

    def dout(name, shape, dt=F32):
        return nc.dram_tensor(name, list(shape), dt, kind="ExternalOutput").ap()

    NW, NM = n_warm * NT, n_main * NT
    xw = din("xw", [max(NW, 1), D])
    xm = din("xm", [NM, D])
    xs = din("xs", [64, D])
    st0 = din("st0", [128, 8, 64])
    sh0 = din("sh0", [128, 8])
    ck = din("ck", [128, 256])
    cv = din("cv", [128, 256])
    masku_d = din("masku", [4, 128])
    maskv_d = din("maskv", [4, 2, 256])
    ropew = din("ropew", [16, 2, NT])
    ropem = din("ropem", [16, 2, NM])
    ropes = din("ropes", [16, 2, 64])
    vecs_d = din("vecs", [128, NVEC, 8])
    cst_d = din("cst", [128, CS_W])
    bq_d = din("bq", [64, 16])
    bk_d = din("bk", [64, 4])
    bvb_d = din("bvb", [128, 256])
    sink_d = din("sinkb", [128, 16])
    w_rkv = din("rw_w_rkv", [3, D, D])
    w_1 = din("rw_w1", [D, 64])
    w_2 = din("rw_w2", [64, D])
    a_1 = din("rw_a1", [D, 64])
    a_2 = din("rw_a2", [64, D])
    g_1 = din("rw_g1", [D, 160])
    g_2 = din("rw_g2", [160, D])
    w_o0 = din("rw_w_o", [D, D])
    w_kv = din("w_kv", [D, 512])
    w_q = din("w_q", [D, D])
    w_o1 = din("w_o", [D, D])
    f_in = din("ffn_w_in", [2, D, 2 * DFF])
    f_out = din("ffn_w_out", [2, DFF, D])

    y_m = dout("y_m", [NM, D])
    y_s = dout("y_s", [64, D])
    st_m = dout("st_m", [128, 8, 64])
    sh_m = dout("sh_m", [128, 8])
    ck_m = dout("ck_m", [128, 256])
    cv_m = dout("cv_m", [128, 256])
    st_s = dout("st_s", [128, 8, 64])
    sh_s = dout("sh_s", [128, 8])
    ck_s = dout("ck_s", [128, 256])
    cv_s = dout("cv_s", [128, 256])
    dbg_outs = {}

    SLOT = 4096
    blocks = {}
    order = []

    def defblock(name, E):
        blocks[name] = {"E": E, "idx": len(order)}
        order.append(name)

    for nm in ("wk", "wv", "wr"):
        defblock(nm + "0", 4096)
        defblock(nm + "1", 4096)
    defblock("wo0", 4096)
    defblock("wo1", 4096)
    for l in range(2):
        for j in range(11):
            defblock(f"f{l}i{j}", 4096)
        for oc in range(8):
            defblock(f"f{l}o{oc}", NJ * 128)
    defblock("wkv", 4096)
    defblock("wq0", 4096)
    defblock("wq1", 4096)
    defblock("wp0", 4096)
    defblock("wp1", 4096)
    nblk_w = len(order)
    wsc = nc.dram_tensor("wsc", [nblk_w, 128, SLOT], BF16, kind="Internal").ap()
    wsc_buf = Buf("wsc")

    with es:
        def sem(name):
            return es.enter_context(nc.semaphore(name))

        def sbt(name, shape, dt=F32):
            return TB(es.enter_context(nc.sbuf_tensor("s_" + name, list(shape), dt)), name)

        P_pe = Prod("pe", sem("s_pe"))
        P_act = Prod("act", sem("s_act"))
        P_dve = Prod("dve", sem("s_dve"))
        P_pool = Prod("pool", sem("s_pool"))
        PE = Eng("pe", P_pe)
        ACT = Eng("act", P_act)
        DVE = Eng("dve", P_dve)
        POOL = Eng("pool", P_pool)
        SP = Eng("sp", None)
        prods = [P_pe, P_act, P_dve, P_pool]

        def dsem(name):
            p = Prod(name, sem(name), is_dma=True)
            prods.append(p)
            return p

        xT = sbt("xT", [128, 8, NT])
        NRING = 4
        ring = [sbt(f"ring{i}", [128, SLOT], BF16) for i in range(NRING)]
        ring_sem = [dsem(f"d_ring{i}") for i in range(NRING)]
        ring_state = {"i": 0}
        w1s = sbt("w1s", [128, 8, 64], BF16)
        a1s = sbt("a1s", [128, 8, 64], BF16)
        g1s = sbt("g1s", [128, 8, 160], BF16)
        w2a2 = sbt("w2a2", [128, D], BF16)
        g2s = sbt("g2s", [128, 2, D], BF16)
        vecs = sbt("vecs", [128, NVEC, 8])
        cst = sbt("cst", [128, CS_W])
        cb = sbt("cb", [128, 128 * 3 + 64 + 16], BF16)
        mkb = sbt("mkb", [4, 128 + 512], BF16)
        bq = sbt("bq", [64, 16])
        bk = sbt("bk", [64, 4])
        bvb = sbt("bvb", [128, 256])
        sinkb = sbt("sinkb", [128, 16])
        ST = sbt("ST", [128, 8, 64])
        STb = sbt("STb", [128, 8, 64], BF16)
        hprev = sbt("hprev", [128, 8])
        KF = sbt("KF", [64, 4, 128 + NT], BF16)
        Vt = sbt("Vt", [128, 1 + NT // 128, 256], BF16)
        Abig = es.enter_context(nc.sbuf_tensor("s_Abig", [128, 5, 8, NT], F32))
        A = [TB(Abig[:, i], f"A{i}") for i in range(5)]
        Bz = [sbt(f"B{i}", [128, 8, NT], BF16) for i in range(7)]
        masku = TB(A[0].t[0:4, 0, 0:128], "masku")
        masku.b = A[0].b
        maskv = TB(A[0].t[0:4, 1:3, :], "maskv")
        maskv.b = A[0].b
        xin = TB(A[4].t[:, :, :].rearrange("p (b a) n -> p b (a n)", b=2), "xin")
        xin.b = A[4].b
        ropet = TB(A[3].t[0:16, 6:8, :], "ropet")
        ropet.b = A[3].b
        hid_ap = Abig[:, 0:2].rearrange("p a b n -> p (a b n)").bitcast(BF16)[:, 0:NJ * NT].rearrange(
            "p (j n) -> p j n", n=NT)
        hid_bufs = [A[0].b, A[1].b]
        AR = sbt("AR", [128, 8, NT // 64, 2, 64], BF16)
        KBD = sbt("KBD", [128, 8, NT // 64, 2, 64], BF16)
        BBD = sbt("BBD", [128, 8, NT // 64, 2, 64], BF16)
        rows = [sbt("row0", [128, NT])]
        gC = sbt("gC", [128, 8, NT // 64])
        t1ab = sbt("t1ab", [128, NT], BF16)
        tgb = sbt("tgb", [128, 2, NT], BF16)
        KtmBD = [sbt(f"KtmBD{i}", [128, 8, 128], BF16) for i in range(2)]
        BtmBD = [sbt(f"BtmBD{i}", [128, 8, 128], BF16) for i in range(2)]
        Vtm = [sbt(f"Vtm{i}", [128, 8, 64], BF16) for i in range(2)]
        Aak = [sbt(f"Aak{i}", [128, 8, 64], BF16) for i in range(2)]
        Akr = [sbt(f"Akr{i}", [128, 8, 64], BF16) for i in range(2)]
        Abr = [sbt(f"Abr{i}", [128, 8, 64], BF16) for i in range(2)]
        TBD = [sbt(f"TBD{i}", [128, 8, 128], BF16) for i in range(2)]
        TBk = [sbt(f"TBk{i}", [128, 8, 128], BF16) for i in range(2)]
        PBD = [sbt(f"PBD{i}", [128, 8, 128], BF16) for i in range(2)]
        PTBD = [sbt(f"PTBD{i}", [128, 8, 128], BF16) for i in range(2)]
        X0b = sbt("X0b", [128, 8, 64], BF16)
        Xb = sbt("Xb", [128, 8, 64], BF16)
        Stmp = TB(A[3].t[:, 0:2, :].rearrange("p a (b v) -> p (a b) v", v=64), "Stmp")
        Stmp.b = A[3].b
        gns = sbt("gns", [128, 6, 8 * (NT // 64)])
        Eb = [sbt(f"Eb{i}", [128, 4, 256], BF16) for i in range(3)]
        pTb = [sbt(f"pTb{i}", [128, 4, 2, 128], BF16) for i in range(2)]
        asm = [sbt(f"asm{i}", [128, 6, 4]) for i in range(3)]
        asa = [sbt(f"asa{i}", [128, 2, 4]) for i in range(3)]
        Kf = TB(A[2].t[0:64, 0:4, :], "Kf")
        Kf.b = A[2].b
        K16 = TB(Bz[4].t[0:16, 0:4, :], "K16")
        K16.b = Bz[4].b
        r16 = TB(A[3].t[0:16, 0:6, :].rearrange("p (k a) n -> p k (a n)", k=3), "r16")
        r16.b = A[3].b
        kvo = TB(A[4].t[:, 4:6, :], "kvo")
        kvo.b = A[4].b
        banks = [TB(es.enter_context(nc.psum_tensor(f"bank{i}", [128, 512], F32)), f"bank{i}") for i in range(8)]
        for b_ in banks:
            b_.b.psum = True
        bank_state = {"i": 0}

        d_const = dsem("d_const")
        d_pre = dsem("d_pre")
        d_x = dsem("d_x")
        d_rope = dsem("d_rope")
        d_out = dsem("d_out")
        d_outb = [dsem("d_out0"), dsem("d_out1")]
        d_st = dsem("d_st")

        held = []

        def nb():
            while True:
                b = banks[bank_state["i"] % 8]
                bank_state["i"] += 1
                if b not in held:
                    return b

        def MM(out, lhsT, rhs, R, W, start=True, stop=True):
            S.op(PE, lambda e: e.matmul(out, lhsT, rhs, start=start, stop=stop), R=R, W=W)

        def TR(out, in_, ident, R, W):
            S.op(PE, lambda e: e.transpose(out, in_, ident), R=R, W=W)

        def ACTV(out, in_, func, R, W, bias=None, scale=None, accum=None, relax=False):
            kw = {}
            if bias is not None:
                kw["bias"] = bias
            if scale is not None:
                kw["scale"] = scale
            if accum is not None:
                kw["accum_out"] = accum
            S.op(ACT, lambda e: e.activation(out=out, in_=in_, func=func, **kw), R=R, W=W, relax=relax)

        def TT(E, out, in0, in1, op, R, W, relax=False):
            S.op(E, lambda e: e.tensor_tensor(out=out, in0=in0, in1=in1, op=op), R=R, W=W, relax=relax)

        def TS(E, out, in0, s1, op0, R, W, s2=None, op1=None, relax=False):
            if op1 is None:
                S.op(E, lambda e: e.tensor_scalar(out=out, in0=in0, scalar1=s1, scalar2=None, op0=op0), R=R, W=W,
                     relax=relax)
            else:
                S.op(E, lambda e: e.tensor_scalar(out=out, in0=in0, scalar1=s1, scalar2=s2, op0=op0, op1=op1), R=R, W=W,
                     relax=relax)

        def STT(out, in0, scalar, in1, op0, op1, R, W, relax=False):
            S.op(DVE, lambda e: e.scalar_tensor_tensor(out=out, in0=in0, scalar=scalar, in1=in1, op0=op0, op1=op1),
                 R=R, W=W, relax=relax)

        def CP(E, out, in_, R, W, relax=False):
            if E is ACT:
                S.op(E, lambda e: e.activation(out=out, in_=in_, func=AF.Copy), R=R, W=W, relax=relax)
            else:
                S.op(E, lambda e: e.tensor_copy(out=out, in_=in_), R=R, W=W, relax=relax)

        def RECIP(out, in_, R, W):
            S.op(DVE, lambda e: e.reciprocal(out=out, in_=in_), R=R, W=W)

        def RED(out, in_, op, R, W, relax=False):
            S.op(DVE, lambda e: e.tensor_reduce(out=out, in_=in_, axis=AX.X, op=op), R=R, W=W, relax=relax)

        def MSET(E, ap, val, W):
            S.op(E, lambda e: e.memset(ap, val), R=(), W=W)

        def LOAD(out, in_, ds, W, R=()):
            S.dma(SP, lambda e: e.dma_start(out=out, in_=in_), ds, R=R, W=W)

        pending_stores = []

        def STORE(out, in_, R, ds=None):
            pending_stores.append((out, in_, list(R), ds or d_out))

        def flush_stores():
            for out, in_, R, ds in pending_stores:
                S.dma(SP, (lambda o, i: (lambda e: e.dma_start(out=o, in_=i)))(out, in_), ds, R=R, W=())
            for out, in_, R, ds in pending_stores:
                for b in R:
                    b.r[ds] = ds.n
            pending_stores.clear()

        def dump(name, tb, shape, dt=F32):
            if not dbg:
                return
            o = dout("dbg_" + name, shape, dt)
            dbg_outs[name] = o
            S.dma(SP, lambda e: e.dma_start(out=o, in_=tb.t[:]), d_out, R=[tb.b], W=())

        def cbv(off, n, rows=128):
            return cb[0:rows, off:off + n]

        ident_b = cb[:, 0:128]
        blk_b = cb[:, 128:256]
        ones_b = cb[:, 256:384]
        i64_b = cb[:, 384:448]
        p16_b = cb[0:16, 448:464]
        ident_f = cst[:, CS_ID:CS_ID + 128]

        for (dst, src) in ((vecs, vecs_d), (cst, cst_d), (masku, masku_d), (maskv, maskv_d), (bq, bq_d), (bk, bk_d),
                           (bvb, bvb_d), (sinkb, sink_d)):
            LOAD(dst.t[:], src, d_const, W=[dst.b])
        for dst in (vecs, cst, masku, maskv, bq, bk, bvb, sinkb):
            dst.b.w = (d_const, d_const.n)
        CP(DVE, mkb[:, 0:128], masku.t, R=[masku.b], W=[mkb.b])
        CP(DVE, mkb[:, 128:640].rearrange("p (a n) -> p a n", a=2), maskv.t, R=[maskv.b], W=[mkb.b])
        TS(DVE, vecs[:, V_C1, :], vecs[:, V_KA, :], -1.0, ALU.mult, R=[vecs.b], W=[vecs.b], s2=1.0, op1=ALU.add)

        def PCAST(out, in_, W=()):
            S.dma(POOL, lambda e: e.dma_start(out=out, in_=in_), d_pre, R=(), W=list(W))

        PCAST(w1s.t[:], w_1.rearrange("(k p) c -> p k c", p=128), [w1s.b])
        PCAST(a1s.t[:], a_1.rearrange("(k p) c -> p k c", p=128), [a1s.b])
        PCAST(g1s.t[:], g_1.rearrange("(k p) c -> p k c", p=128), [g1s.b])
        for h in range(2):
            PCAST(w2a2.t[0:64, :].rearrange("p (i h d) -> p i h d", i=8, h=2)[:, :, h, :],
                  w_2.rearrange("p (h i d) -> p h i d", h=2, i=8)[:, h], [w2a2.b])
            PCAST(w2a2.t[64:128, :].rearrange("p (i h d) -> p i h d", i=8, h=2)[:, :, h, :],
                  a_2.rearrange("p (h i d) -> p h i d", h=2, i=8)[:, h], [w2a2.b])
            PCAST(g2s.t[:, 0, :].rearrange("p (i h d) -> p i h d", i=8, h=2)[:, :, h, :],
                  g_2[0:128, :].rearrange("p (h i d) -> p h i d", h=2, i=8)[:, h], [g2s.b])
            PCAST(g2s.t[0:32, 1, :].rearrange("p (i h d) -> p i h d", i=8, h=2)[:, :, h, :],
                  g_2[128:160, :].rearrange("p (h i d) -> p h i d", h=2, i=8)[:, h], [g2s.b])

        def scr(name):
            return wsc[blocks[name]["idx"]]

        pre_sems = [dsem(f"d_prq{i}") for i in range(8)]
        pc_state = {"j": 0, "hist": []}
        pc_queue = []
        for nm_ in order:
            blocks[nm_]["bufs"] = []

        def QCAST(name, out, in_):
            pc_queue.append((name, out, in_))

        def pc_emit(n):
            while n > 0 and pc_queue:
                n -= 1
                name, out, in_ = pc_queue.pop(0)
                j = pc_state["j"]
                pc_state["j"] += 1
                sm = pre_sems[j % 8]
                if j >= 4:
                    ps_, pv_ = pc_state["hist"][j - 4]
                    POOL.ops.append({"k": "w", "p": ps_, "v": pv_})
                b_ = Buf(f"pc{j}")
                S.dma(POOL, (lambda o, i: (lambda e: e.dma_start(out=o, in_=i)))(out, in_), sm, R=(), W=[b_])
                pc_state["hist"].append((sm, sm.n))
                blocks[name]["bufs"].append(b_)

        for j, nm in ((1, "wk"), (2, "wv"), (0, "wr")):
            src = w_rkv[j].rearrange("(k p) (h c) -> p k h c", p=128, h=2)
            for b in range(2):
                dstv = scr(nm + str(b)).rearrange("p (k i h d) -> p k i h d", k=8, i=4, h=2)
                for i4 in range(4):
                    for h in range(2):
                        QCAST(nm + str(b), dstv[:, :, i4, h, :], src[:, :, h, (4 * b + i4) * 64:(4 * b + i4 + 1) * 64])
        n_first = len(pc_queue) - 16
        src = w_o0.rearrange("(h i d) c -> h d i c", h=2, i=8)
        for b in range(2):
            for h in range(2):
                QCAST(f"wo{b}", scr(f"wo{b}")[h * 64:(h + 1) * 64, :].rearrange("p (i c) -> p i c", i=8),
                      src[h, :, :, b * 512:(b + 1) * 512])

        def q_ffn(l):
            src = f_in[l].rearrange("(k p) (g c) -> p k g c", p=128, g=2)
            for j in range(11):
                for g in range(2):
                    QCAST(f"f{l}i{j}", scr(f"f{l}i{j}").rearrange("p (k g c) -> p k g c", k=8, g=2)[:, :, g, :],
                          src[:, :, g, j * 256:(j + 1) * 256])
            src = f_out[l].rearrange("(j p) c -> p j c", p=128)
            for oc in range(8):
                QCAST(f"f{l}o{oc}", scr(f"f{l}o{oc}")[:, 0:NJ * 128].rearrange("p (j c) -> p j c", j=NJ),
                      src[:, :, oc * 128:(oc + 1) * 128])

        q_ffn(0)
        QCAST("wkv", scr("wkv").rearrange("p (k c) -> p k c", k=8), w_kv.rearrange("(k p) c -> p k c", p=128))
        for b in range(2):
            QCAST(f"wq{b}", scr(f"wq{b}").rearrange("p (k c) -> p k c", k=8),
                  w_q.rearrange("(k p) c -> p k c", p=128)[:, :, b * 512:(b + 1) * 512])
        for b in range(2):
            QCAST(f"wp{b}", scr(f"wp{b}").rearrange("p (k c) -> p k c", k=8),
                  w_o1.rearrange("(k p) c -> p k c", p=128)[:, :, b * 512:(b + 1) * 512])
        q_ffn(1)
        pc_emit(n_first if n_warm > 1 else len(pc_queue))
        pc_per_tile = (len(pc_queue) + max(n_warm - 2, 1) - 1) // max(n_warm - 2, 1) if n_warm > 1 else 0

        for dst in (w1s, a1s, g1s, w2a2, g2s):
            dst.b.w = (d_pre, d_pre.n)
        CP(DVE, cb[:, 0:128], cst[:, CS_ID:CS_ID + 128], R=[cst.b], W=[cb.b])
        CP(DVE, cb[:, 128:256], cst[:, CS_BLK:CS_BLK + 128], R=[cst.b], W=[cb.b])
        MSET(DVE, cb[:, 256:384], 1.0, W=[cb.b])
        CP(DVE, cb[:, 384:448], cst[:, CS_I64:CS_I64 + 64], R=[cst.b], W=[cb.b])
        CP(DVE, cb[0:16, 448:464], cst[0:16, CS_P16:CS_P16 + 16], R=[cst.b], W=[cb.b])
        MSET(POOL, KBD.t[:], 0.0, W=[KBD.b])
        MSET(POOL, BBD.t[:], 0.0, W=[BBD.b])
        MSET(POOL, ST.t[:], 0.0, W=[ST.b])
        MSET(POOL, STb.t[:], 0.0, W=[STb.b])
        MSET(POOL, hprev.t[:], 0.0, W=[hprev.b])
        MSET(POOL, KF.t[:], 0.0, W=[KF.b])
        MSET(POOL, Vt.t[:], 0.0, W=[Vt.b])
        MSET(POOL, tgb.t[:], 0.0, W=[tgb.b])

        def wload(name):
            i = ring_state["i"] % NRING
            ring_state["i"] += 1
            E = blocks[name]["E"]
            assert blocks[name]["bufs"], name
            LOAD(ring[i].t[:, 0:E], scr(name)[:, 0:E], ring_sem[i], W=[ring[i].b], R=blocks[name]["bufs"])
            return ring[i]

        def load_x(src, ntok):
            nblk = (ntok + 127) // 128
            for blk in range(nblk):
                tb = min(128, ntok - blk * 128)
                LOAD(xin[0:tb, blk, :], src[blk * 128:blk * 128 + tb, :], d_x, W=[xin.b])
            for kc in range(0, 8, 2):
                bk_ = nb()
                for k2 in range(2):
                    for blk in range(nblk):
                        tb = min(128, ntok - blk * 128)
                        TR(bk_[:, k2 * 256 + blk * 128:k2 * 256 + blk * 128 + tb],
                           xin[0:tb, blk, (kc + k2) * 128:(kc + k2 + 1) * 128], ident_f[0:tb, 0:tb],
                           R=[xin.b, cst.b], W=[bk_.b])
                CP(ACT, xT[:, kc:kc + 2, 0:ntok], bk_[:, :].rearrange("p (a n) -> p a n", a=2)[:, :, 0:ntok],
                   R=[bk_.b], W=[xT.b], relax=kc > 0)

        def rstd_of_x(ntok, sqbuf, rstd):
            ACTV(sqbuf[:, :, 0:ntok], xT[:, :, 0:ntok], AF.Square, R=[xT.b], W=[sqbuf.b])
            bk_ = nb()
            for kc in range(8):
                MM(bk_[:, 0:ntok], ones_b, sqbuf[:, kc, 0:ntok], R=[cb.b, sqbuf.b], W=[bk_.b], start=kc == 0,
                   stop=kc == 7)
            TS(DVE, rstd[:, 0:ntok], bk_[:, 0:ntok], 1.0 / D, ALU.mult, R=[bk_.b], W=[rstd.b], s2=RMS_EPS,
               op1=ALU.add)
            ACTV(rstd[:, 0:ntok], rstd[:, 0:ntok], AF.Sqrt, R=[rstd.b], W=[rstd.b])
            RECIP(rstd[:, 0:ntok], rstd[:, 0:ntok], R=[rstd.b], W=[rstd.b])

        def apply_norm(ntok, rstd, vi, out):
            for kc in range(8):
                STT(out[:, kc, 0:ntok], xT[:, kc, 0:ntok], vecs[:, vi, kc:kc + 1], rstd[:, 0:ntok], ALU.mult,
                    ALU.mult, R=[xT.b, vecs.b, rstd.b], W=[out.b], relax=kc > 0)

        def proj8(ntok, blkname, rhs_tb, evac):
            for b in range(2):
                slot = wload(blkname + str(b))
                sv = slot.t[:, :].rearrange("p (k c) -> p k c", k=8)
                for i2 in range(2):
                    bk_ = nb()
                    for ii in range(2):
                        i4 = i2 * 2 + ii
                        for kc in range(8):
                            MM(bk_[:, ii * 256:ii * 256 + ntok], sv[:, kc, i4 * 128:(i4 + 1) * 128],
                               rhs_tb[:, kc, 0:ntok], R=[slot.b, rhs_tb.b], W=[bk_.b], start=kc == 0, stop=kc == 7)
                    evac(bk_, 4 * b + i2 * 2)

        def b2(bk_, ntok):
            return bk_[:, :].rearrange("p (a n) -> p a n", a=2)[:, :, 0:ntok]

        def time_mix(ntok, mode):
            nch = ntok // 64
            full = mode != "warm"
            hbuf, xx, kf, sg, av = A
            xl = Bz[0:2]
            Vfm = Bz[2]
            G = Bz[3]
            rf = Bz[5]
            kk = Bz[6]
            rstd = rows[0]
            rstd_of_x(ntok, Bz[4], rstd)
            apply_norm(ntok, rstd, V_NM0, hbuf)
            TT(DVE, xx[:, :, 1:ntok], hbuf[:, :, 0:ntok - 1], hbuf[:, :, 1:ntok], ALU.subtract, R=[hbuf.b],
               W=[xx.b])
            TT(DVE, xx[:, :, 0:1], hprev[:, :].unsqueeze(2), hbuf[:, :, 0:1], ALU.subtract, R=[hbuf.b, hprev.b],
               W=[xx.b])
            CP(ACT, hprev[:, :].unsqueeze(2), hbuf[:, :, ntok - 1:ntok], R=[hbuf.b], W=[hprev.b])
            stage()
            st = {"i": 0}

            def lerp(j):
                o = xl[st["i"] % 2]
                st["i"] += 1
                for kc in range(8):
                    STT(o[:, kc, 0:ntok], xx[:, kc, 0:ntok], vecs[:, V_MU0 + j, kc:kc + 1], hbuf[:, kc, 0:ntok],
                        ALU.mult, ALU.add, R=[xx.b, vecs.b, hbuf.b], W=[o.b], relax=kc > 0)
                return o

            o = lerp(2)
            proj8(ntok, "wk", o, lambda bk_, i0: CP(ACT, kf[:, i0:i0 + 2, 0:ntok], b2(bk_, ntok), R=[bk_.b],
                                                     W=[kf.b], relax=i0 > 0))
            stage()
            o = lerp(3)
            proj8(ntok, "wv", o, lambda bk_, i0: CP(ACT, Vfm[:, i0:i0 + 2, 0:ntok], b2(bk_, ntok), R=[bk_.b],
                                                     W=[Vfm.b], relax=i0 > 0))
            if full:
                o = lerp(0)
                proj8(ntok, "wr", o, lambda bk_, i0: CP(ACT, rf[:, i0:i0 + 2, 0:ntok], b2(bk_, ntok), R=[bk_.b],
                                                         W=[rf.b], relax=i0 > 0))
            stage()
            o = lerp(1)
            bk_ = nb()
            for kc in range(8):
                MM(bk_[0:64, 0:ntok], w1s[:, kc, :], o[:, kc, 0:ntok], R=[w1s.b, o.b], W=[bk_.b], start=kc == 0,
                   stop=kc == 7)
            ACTV(t1ab[0:64, 0:ntok], bk_[0:64, 0:ntok], AF.Tanh, R=[bk_.b], W=[t1ab.b])
            o = lerp(4)
            bk_ = nb()
            for kc in range(8):
                MM(bk_[64:128, 0:ntok], a1s[:, kc, :], o[:, kc, 0:ntok], R=[a1s.b, o.b], W=[bk_.b], start=kc == 0,
                   stop=kc == 7)
            CP(ACT, t1ab[64:128, 0:ntok], bk_[64:128, 0:ntok], R=[bk_.b], W=[t1ab.b])
            if full:
                o = lerp(5)
                bk_ = nb()
                for kc in range(8):
                    MM(bk_[:, 0:ntok], g1s[:, kc, 0:128], o[:, kc, 0:ntok], R=[g1s.b, o.b], W=[bk_.b],
                       start=kc == 0, stop=kc == 7)
                for kc in range(8):
                    MM(bk_[0:32, 256:256 + ntok], g1s[:, kc, 128:160], o[:, kc, 0:ntok], R=[g1s.b, o.b],
                       W=[bk_.b], start=kc == 0, stop=kc == 7)
                ACTV(tgb[:, 0, 0:ntok], bk_[:, 0:ntok], AF.Sigmoid, R=[bk_.b], W=[tgb.b])
                ACTV(tgb[0:32, 1, 0:ntok], bk_[0:32, 256:256 + ntok], AF.Sigmoid, R=[bk_.b], W=[tgb.b])
            stage()
            for i in range(0, 8, 2):
                bkW = nb()
                bkA_ = nb()
                for ii in range(2):
                    cs_ = slice((i + ii) * 128, (i + ii + 1) * 128)
                    MM(bkW[:, ii * 256:ii * 256 + ntok], w2a2[0:64, cs_], t1ab[0:64, 0:ntok], R=[w2a2.b, t1ab.b],
                       W=[bkW.b])
                    MM(bkA_[:, ii * 256:ii * 256 + ntok], w2a2[64:128, cs_], t1ab[64:128, 0:ntok],
                       R=[w2a2.b, t1ab.b], W=[bkA_.b])
                for ii in range(2):
                    ACTV(sg[:, i + ii, 0:ntok], bkW[:, ii * 256:ii * 256 + ntok], AF.Sigmoid, R=[bkW.b, vecs.b],
                         W=[sg.b], bias=vecs[:, V_W0, i + ii:i + ii + 1], relax=(i + ii) > 0)
                    ACTV(av[:, i + ii, 0:ntok], bkA_[:, ii * 256:ii * 256 + ntok], AF.Sigmoid, R=[bkA_.b, vecs.b],
                         W=[av.b], bias=vecs[:, V_A0, i + ii:i + ii + 1], relax=(i + ii) > 0)
            if full:
                for i in range(8):
                    cs_ = slice(i * 128, (i + 1) * 128)
                    bk2 = nb()
                    MM(bk2[:, 0:ntok], g2s[:, 0, cs_], tgb[:, 0, 0:ntok], R=[g2s.b, tgb.b], W=[bk2.b], start=True,
                       stop=False)
                    MM(bk2[:, 0:ntok], g2s[0:32, 1, cs_], tgb[0:32, 1, 0:ntok], R=[g2s.b, tgb.b], W=[bk2.b],
                       start=False, stop=True)
                    CP(ACT, G[:, i, 0:ntok], bk2[:, 0:ntok], R=[bk2.b], W=[G.b], relax=i > 0)
            stage()
            sqk = Bz[4]
            for i in range(8):
                ACTV(sqk[:, i, 0:ntok], kf[:, i, 0:ntok], AF.Square, R=[kf.b, vecs.b], W=[sqk.b],
                     scale=vecs[:, V_KK, i:i + 1], relax=i > 0)
            sdk = xx
            for i in range(0, 8, 2):
                bk_ = nb()
                for ii in range(2):
                    MM(bk_[:, ii * 256:ii * 256 + ntok], blk_b, sqk[:, i + ii, 0:ntok], R=[cb.b, sqk.b], W=[bk_.b])
                TS(DVE, sdk[:, i:i + 2, 0:ntok], b2(bk_, ntok), 1e-24, ALU.max, R=[bk_.b], W=[sdk.b], relax=i > 0)
            ACTV(sdk[:, :, 0:ntok], sdk[:, :, 0:ntok], AF.Ln, R=[sdk.b], W=[sdk.b], scale=float(2.0 ** 40))
            ACTV(sdk[:, :, 0:ntok], sdk[:, :, 0:ntok], AF.Exp, R=[sdk.b], W=[sdk.b], scale=-0.5,
                 bias=float(20.0 * np.log(2.0)))
            for i in range(8):
                STT(kk[:, i, 0:ntok], kf[:, i, 0:ntok], vecs[:, V_KK, i:i + 1], sdk[:, i, 0:ntok], ALU.mult, ALU.mult,
                    R=[kf.b, vecs.b, sdk.b], W=[kk.b], relax=i > 0)
            stage()
            cs = hbuf
            for i in range(8):
                S.op(DVE, (lambda i: lambda e: e.tensor_tensor_scan(
                    out=cs[:, i, 0:ntok], data0=cst[:, CS_RESET:CS_RESET + ntok], data1=sg[:, i, 0:ntok],
                    initial=0.0, op0=ALU.mult, op1=ALU.add))(i), R=[cst.b, sg.b], W=[cs.b], relax=i > 0)
            stage()
            gam = sg
            gami = xx
            ACTV(gam[:, :, 0:ntok], cs[:, :, 0:ntok], AF.Exp, R=[cs.b], W=[gam.b], scale=-C0)
            ACTV(gami[:, :, 0:ntok], cs[:, :, 0:ntok], AF.Exp, R=[cs.b], W=[gami.b], scale=C0)
            CP(ACT, gC[:, :, 0:nch], gam[:, :, 63:ntok:64], R=[gam.b], W=[gC.b])
            stage()
            u = hbuf
            for i in range(8):
                ACTV(u[:, i, 0:ntok], av[:, i, 0:ntok], AF.Identity, R=[av.b, vecs.b, gam.b, gami.b], W=[u.b],
                     scale=vecs[:, V_KA, i:i + 1], bias=vecs[:, V_C1, i:i + 1], relax=i > 0)
            TT(DVE, kf[:, :, 0:ntok], kf[:, :, 0:ntok], u[:, :, 0:ntok], ALU.mult, R=[kf.b, u.b], W=[kf.b])
            TT(DVE, av[:, :, 0:ntok], kk[:, :, 0:ntok], av[:, :, 0:ntok], ALU.mult, R=[kk.b, av.b], W=[av.b])
            if full:
                rk = u
                TT(DVE, rk[:, :, 0:ntok], rf[:, :, 0:ntok], bc(vecs[:, V_RK, :].unsqueeze(2), [128, 8, ntok]),
                   ALU.mult, R=[rf.b, vecs.b, u.b], W=[rk.b])
                rkb = Bz[4]
                TT(DVE, rkb[:, :, 0:ntok], rk[:, :, 0:ntok], kf[:, :, 0:ntok], ALU.mult, R=[rk.b, kf.b], W=[rkb.b])
                bonus = hbuf
                for i in range(0, 8, 2):
                    bk_ = nb()
                    for ii in range(2):
                        MM(bk_[:, ii * 256:ii * 256 + ntok], blk_b, rkb[:, i + ii, 0:ntok], R=[cb.b, rkb.b],
                           W=[bk_.b])
                    TT(DVE, bonus[:, i:i + 2, 0:ntok], b2(bk_, ntok), Vfm[:, i:i + 2, 0:ntok], ALU.mult,
                       R=[bk_.b, Vfm.b, rk.b], W=[bonus.b], relax=i > 0)
                TT(DVE, AR[:, :, 0:nch, 1, :], rf[:, :, 0:ntok].rearrange("p a (c t) -> p a c t", t=64),
                   gam[:, :, 0:ntok].rearrange("p a (c t) -> p a c t", t=64), ALU.mult, R=[rf.b, gam.b], W=[AR.b])
            stage()
            kkv = kk[:, :, 0:ntok].rearrange("p a (c t) -> p a c t", t=64)
            gmv = gam[:, :, 0:ntok].rearrange("p a (c t) -> p a c t", t=64)
            arv = AR[:, :, 0:nch, 0, :]
            STT(arv[:, :, :, 1:64], kkv[:, :, :, 1:64], -1.0, gmv[:, :, :, 0:63], ALU.mult, ALU.mult,
                R=[kk.b, gam.b], W=[AR.b])
            TS(DVE, arv[:, :, :, 0:1], kkv[:, :, :, 0:1], -1.0, ALU.mult, R=[kk.b], W=[AR.b])
            stage()
            kpv = kf[:, :, 0:ntok].rearrange("p a (c t) -> p a c t", t=64)
            bv = av[:, :, 0:ntok].rearrange("p a (c t) -> p a c t", t=64)
            giv = gami[:, :, 0:ntok].rearrange("p a (c t) -> p a c t", t=64)
            for h in range(2):
                ps = slice(h * 64, (h + 1) * 64)
                TT(DVE, KBD[ps, :, 0:nch, h, :], kpv[ps], giv[ps], ALU.mult, R=[kf.b, gami.b], W=[KBD.b])
                TT(DVE, BBD[ps, :, 0:nch, h, :], bv[ps], giv[ps], ALU.mult, R=[av.b, gami.b], W=[BBD.b])
            return (hbuf if full else None), G, Vfm

        def bfv(bk_):
            return bk_[:, :].bitcast(BF16)

        def wkv_part1(c, par, Vfm, full):
            Kt, Bt, Vm = KtmBD[par], BtmBD[par], Vtm[par]
            cs_ = slice(c * 64, (c + 1) * 64)
            bkK = nb()
            bkB = nb()
            for i in range(8):
                TR(bfv(bkK)[:, i * 128:(i + 1) * 128], KBD[:, i, c, :, :].rearrange("p h t -> p (h t)"), ident_b,
                   R=[KBD.b, cb.b], W=[bkK.b])
            CP(ACT, Kt.t[:, :, :].rearrange("p a b -> p (a b)"), bfv(bkK), R=[bkK.b], W=[Kt.b])
            for i in range(8):
                TR(bfv(bkB)[:, i * 128:(i + 1) * 128], BBD[:, i, c, :, :].rearrange("p h t -> p (h t)"), ident_b,
                   R=[BBD.b, cb.b], W=[bkB.b])
            CP(DVE, Bt.t[:, :, :].rearrange("p a b -> p (a b)"), bfv(bkB), R=[bkB.b], W=[Bt.b])
            yield
            bkV = [nb(), nb()]
            for i in range(8):
                for h in range(2):
                    ps = slice(h * 64, (h + 1) * 64)
                    TR(bfv(bkV[h])[ps, i * 64:(i + 1) * 64], Vfm[ps, i, cs_], cb[ps, h * 64:(h + 1) * 64],
                       R=[Vfm.b, cb.b], W=[bkV[h].b])
            for h in range(2):
                ps = slice(h * 64, (h + 1) * 64)
                CP(ACT, Vm.t[ps, :, :].rearrange("p a b -> p (a b)"), bfv(bkV[h])[ps, 0:512],
                   R=[bkV[h].b], W=[Vm.b], relax=h > 0)
            yield
            ncol = 128 if full else 64
            msbd = cst[:, CS_MSBD:CS_MSBD + 128].rearrange("p (h t) -> p h t", h=2)
            mstbd = cst[:, CS_MSTBD:CS_MSTBD + 128].rearrange("p (h t) -> p h t", h=2)
            ms = cst[:, CS_MS:CS_MS + 64]
            mi = cst[:, CS_MI:CS_MI + 64]

            def bdsrc(bk_, off):
                v = bk_[:, :].rearrange("p (a n) -> p a n", a=4)[:, :, off:off + 64]
                return bc(v.unsqueeze(2), [128, 4, 2, 64])

            def bdmask(m):
                return bc(m.unsqueeze(1), [128, 4, 2, 64])

            for g4 in range(2):
                bkA = nb()
                for ii in range(4):
                    i = g4 * 4 + ii
                    rhs = AR[:, i, c, 0:(2 if full else 1), :].rearrange("p a t -> p (a t)")
                    MM(bkA[:, ii * 128:ii * 128 + ncol], KBD[:, i, c, :, :].rearrange("p h t -> p (h t)"), rhs,
                       R=[KBD.b, AR.b], W=[bkA.b])
                o4 = slice(g4 * 4, g4 * 4 + 4)
                vA = bkA[:, :].rearrange("p (a n) -> p a n", a=4)
                TT(DVE, Aak[par].t[:, o4, :], vA[:, :, 0:64], bc(ms.unsqueeze(1), [128, 4, 64]), ALU.mult,
                   R=[bkA.b, cst.b], W=[Aak[par].b])
                if full:
                    TT(DVE, Akr[par].t[:, o4, :], vA[:, :, 64:128], bc(mi.unsqueeze(1), [128, 4, 64]), ALU.mult,
                       R=[bkA.b, cst.b], W=[Akr[par].b])
                bkB2 = nb()
                for ii in range(4):
                    i = g4 * 4 + ii
                    rhs = AR[:, i, c, 0:(2 if full else 1), :].rearrange("p a t -> p (a t)")
                    MM(bkB2[:, ii * 128:ii * 128 + ncol], BBD[:, i, c, :, :].rearrange("p h t -> p (h t)"), rhs,
                       R=[BBD.b, AR.b], W=[bkB2.b])
                if full:
                    TT(DVE, Abr[par].t[:, o4, :], bkB2[:, :].rearrange("p (a n) -> p a n", a=4)[:, :, 64:128],
                       bc(mi.unsqueeze(1), [128, 4, 64]), ALU.mult, R=[bkB2.b, cst.b], W=[Abr[par].b])
                TT(DVE, PBD[0].t[:, o4, :].rearrange("p a (h t) -> p a h t", h=2), bdsrc(bkB2, 0), bdmask(msbd),
                   ALU.mult, R=[bkB2.b, cst.b], W=[PBD[0].b])
                yield
            bkN = [nb(), nb()]
            for i in range(8):
                for h in range(2):
                    ps = slice(h * 64, (h + 1) * 64)
                    MM(bkN[h][ps, i * 64:(i + 1) * 64], AR[ps, i, c, 0, :], BBD[ps, i, c, h, :], R=[AR.b, BBD.b],
                       W=[bkN[h].b])
            for h in range(2):
                ps = slice(h * 64, (h + 1) * 64)
                vN = bkN[h][ps, :].rearrange("p (a n) -> p a n", a=8)
                TT(DVE, PTBD[0].t[ps, :, :].rearrange("p a (h t) -> p a h t", h=2),
                   bc(vN.unsqueeze(2), [64, 8, 2, 64]), bc(mstbd[ps].unsqueeze(1), [64, 8, 2, 64]), ALU.mult,
                   R=[bkN[h].b, cst.b], W=[PTBD[0].b])
            yield
            for k in range(6):
                cur, nxt = k % 2, (k + 1) % 2
                for g4 in range(2):
                    o4 = slice(g4 * 4, g4 * 4 + 4)
                    bkT_ = nb()
                    for ii in range(4):
                        i = g4 * 4 + ii
                        rhsT = ident_b if k == 0 else TBk[cur].t[:, i, :]
                        rT = [cb.b] if k == 0 else [TBk[cur].b]
                        MM(bkT_[:, ii * 128:(ii + 1) * 128], ident_b, rhsT, R=[cb.b] + rT, W=[bkT_.b], start=True,
                           stop=False)
                        MM(bkT_[:, ii * 128:(ii + 1) * 128], PTBD[cur].t[:, i, :], rhsT, R=[PTBD[cur].b] + rT,
                           W=[bkT_.b], start=False, stop=True)
                    if k <= 4:
                        bkPT = nb()
                        for ii in range(4):
                            i = g4 * 4 + ii
                            MM(bkPT[:, ii * 128:(ii + 1) * 128], PBD[cur].t[:, i, :], PTBD[cur].t[:, i, :],
                               R=[PBD[cur].b, PTBD[cur].b], W=[bkPT.b])
                    if k <= 3:
                        bkP = nb()
                        for ii in range(4):
                            i = g4 * 4 + ii
                            MM(bkP[:, ii * 128:(ii + 1) * 128], PTBD[cur].t[:, i, :], PBD[cur].t[:, i, :],
                               R=[PTBD[cur].b, PBD[cur].b], W=[bkP.b])
                    Tdst = TBD[par] if k == 5 else TBk[nxt]
                    CP(ACT if k % 2 == 0 else DVE, Tdst.t[:, o4, :].rearrange("p a n -> p (a n)"), bkT_[:, :],
                       R=[bkT_.b], W=[Tdst.b], relax=True)
                    if k <= 4:
                        CP(DVE if k % 2 == 0 else ACT, PTBD[nxt].t[:, o4, :].rearrange("p a n -> p (a n)"), bkPT[:, :],
                           R=[bkPT.b], W=[PTBD[nxt].b], relax=True)
                    if k <= 3:
                        CP(ACT, PBD[nxt].t[:, o4, :].rearrange("p a n -> p (a n)"), bkP[:, :], R=[bkP.b],
                           W=[PBD[nxt].b], relax=True)
                    yield

        def wkv_part2(c, par, full, Ystore):
            Kt, Bt, Vm = KtmBD[par], BtmBD[par], Vtm[par]
            bkX = [nb(), nb()]
            for i in range(8):
                for h in range(2):
                    ps = slice(h * 64, (h + 1) * 64)
                    MM(bkX[h][ps, i * 64:(i + 1) * 64], AR[ps, i, c, 0, :], STb[ps, i, :], R=[AR.b, STb.b],
                       W=[bkX[h].b], start=True, stop=False)
                for h in range(2):
                    ps = slice(h * 64, (h + 1) * 64)
                    MM(bkX[h][ps, i * 64:(i + 1) * 64], Aak[par].t[ps, i, :], Vm.t[ps, i, :], R=[Aak[par].b, Vm.b],
                       W=[bkX[h].b], start=False, stop=True)
            for h in range(2):
                ps = slice(h * 64, (h + 1) * 64)
                CP(ACT if h == 0 else DVE, X0b.t[ps, :, :].rearrange("p a b -> p (a b)"), bkX[h][ps, :],
                   R=[bkX[h].b], W=[X0b.b])
            yield
            bkX2 = nb()
            for i in range(8):
                MM(bkX2[:, i * 64:(i + 1) * 64], TBD[par].t[:, i, :], X0b.t[:, i, :], R=[TBD[par].b, X0b.b],
                   W=[bkX2.b])
            CP(ACT, Xb.t[:, :, :].rearrange("p a b -> p (a b)"), bkX2[:, :], R=[bkX2.b], W=[Xb.b])
            yield
            bkS = nb()
            for i in range(8):
                MM(bkS[:, i * 64:(i + 1) * 64], Bt.t[:, i, :], Xb.t[:, i, :], R=[Bt.b, Xb.b], W=[bkS.b], start=True,
                   stop=False)
                MM(bkS[:, i * 64:(i + 1) * 64], Kt.t[:, i, :], Vm.t[:, i, :], R=[Kt.b, Vm.b], W=[bkS.b], start=False,
                   stop=True)
            if full:
                bkY = [nb(), nb()]
                for i in range(8):
                    for h in range(2):
                        ps = slice(h * 64, (h + 1) * 64)
                        MM(bkY[h][ps, i * 64:(i + 1) * 64], AR[ps, i, c, 1, :], STb[ps, i, :], R=[AR.b, STb.b],
                           W=[bkY[h].b], start=True, stop=False)
                    for h in range(2):
                        ps = slice(h * 64, (h + 1) * 64)
                        MM(bkY[h][ps, i * 64:(i + 1) * 64], Abr[par].t[ps, i, :], Xb.t[ps, i, :], R=[Abr[par].b, Xb.b],
                           W=[bkY[h].b], start=False, stop=False)
                    for h in range(2):
                        ps = slice(h * 64, (h + 1) * 64)
                        MM(bkY[h][ps, i * 64:(i + 1) * 64], Akr[par].t[ps, i, :], Vm.t[ps, i, :], R=[Akr[par].b, Vm.b],
                           W=[bkY[h].b], start=False, stop=True)
            TT(DVE, Stmp.t[:, :, :], bkS[:, :].rearrange("p (a n) -> p a n", a=8), ST.t[:, :, :], ALU.add,
               R=[bkS.b, ST.b], W=[Stmp.b])
            TT(DVE, ST.t[:, :, :], Stmp.t[:, :, :], bc(gC[:, :, c:c + 1], [128, 8, 64]), ALU.mult, R=[Stmp.b, gC.b],
               W=[ST.b])
            CP(DVE, STb.t[:, :, :], ST.t[:, :, :], R=[ST.b], W=[STb.b])
            if full:
                for h in range(2):
                    ps = slice(h * 64, (h + 1) * 64)
                    CP(ACT, Ystore[ps, 2 * c:2 * c + 2, :], bkY[h][ps, :].rearrange("p (a n) -> p a n", a=2),
                       R=[bkY[h].b], W=[Ystore.b], relax=h > 0)
            yield

        def wkv_all(nch, Vfm, full, Ystore):
            def drain(g):
                for _ in g:
                    pass
            p1 = wkv_part1(0, 0, Vfm, full)
            drain(p1)
            for c in range(nch):
                p2 = wkv_part2(c, c % 2, full, Ystore)
                p1 = wkv_part1(c + 1, (c + 1) % 2, Vfm, full) if c + 1 < nch else iter(())
                done1 = done2 = False
                while not (done1 and done2):
                    if not done2:
                        try:
                            next(p2)
                        except StopIteration:
                            done2 = True
                    for _ in range(5):
                        if done1:
                            break
                        try:
                            next(p1)
                        except StopIteration:
                            done1 = True

        def tm_out(ntok, Ystore, bonus, G):
            nch = ntok // 64
            ng = nch * 8
            Yv = Ystore[:, 0:2 * nch, :].rearrange("p a (b v) -> p (a b) v", v=64)
            ysq = A[1]
            ysv = ysq[:, 0:2 * nch, :].rearrange("p a (b v) -> p (a b) v", v=64)
            ACTV(ysv, Yv, AF.Square, R=[Ystore.b], W=[ysq.b])
            s1, s2, mm, var = (gns[:, k, 0:ng] for k in range(4))
            RED(s1, Yv, ALU.add, R=[Ystore.b], W=[gns.b])
            RED(s2, ysv, ALU.add, R=[ysq.b], W=[gns.b])
            TS(DVE, mm, s1, 1.0 / 64, ALU.mult, R=[gns.b], W=[gns.b])
            TT(DVE, var, mm, mm, ALU.mult, R=[gns.b], W=[gns.b])
            STT(var, s2, 1.0 / 64, var, ALU.mult, ALU.subtract, R=[gns.b], W=[gns.b])
            TS(DVE, var, var, LNX_EPS, ALU.add, R=[gns.b], W=[gns.b])
            ACTV(var, var, AF.Sqrt, R=[gns.b], W=[gns.b])
            RECIP(var, var, R=[gns.b], W=[gns.b])
            TT(DVE, ysv, Yv, bc(mm.unsqueeze(2), [128, ng, 64]), ALU.subtract, R=[Ystore.b, gns.b], W=[ysq.b])
            ynb = Bz[4]
            ynv = ynb[:, 0:2 * nch, :].rearrange("p a (b v) -> p (a b) v", v=64)
            TT(DVE, ynv, ysv, bc(var.unsqueeze(2), [128, ng, 64]), ALU.mult, R=[ysq.b, gns.b], W=[ynb.b])
            ZT = Bz[0]
            for g4 in range(2):
                bkh = [nb(), nb()]
                bvh = [bfv(b_).rearrange("p (a n) -> p a n", a=4) for b_ in bkh]
                for ii in range(4):
                    i = g4 * 4 + ii
                    for c in range(nch):
                        for h in range(2):
                            ps = slice(h * 64, (h + 1) * 64)
                            TR(bvh[h][ps, ii, c * 64:(c + 1) * 64],
                               ynb[ps, 2 * c + i // 4, (i % 4) * 64:(i % 4 + 1) * 64], cb[ps, h * 64:(h + 1) * 64],
                               R=[ynb.b, cb.b], W=[bkh[h].b])
                for ii in range(4):
                    i = g4 * 4 + ii
                    for h in range(2):
                        ps = slice(h * 64, (h + 1) * 64)
                        STT(A[2][ps, i, 0:ntok], bvh[h][ps, ii, 0:ntok], vecs[ps, V_LNW, i:i + 1],
                            bonus[ps, i, 0:ntok], ALU.mult, ALU.add, R=[bkh[h].b, vecs.b, bonus.b], W=[A[2].b],
                            relax=(i > 0 or h > 0))
                    STT(ZT[:, i, 0:ntok], A[2][:, i, 0:ntok], vecs[:, V_LNB, i:i + 1], G[:, i, 0:ntok], ALU.add,
                        ALU.mult, R=[A[2].b, vecs.b, G.b], W=[ZT.b], relax=i > 0)
            for b in range(2):
                slot = wload(f"wo{b}")
                sv = slot.t[:, :].rearrange("p (i c) -> p i c", i=8)
                for o2 in range(2):
                    bk_ = nb()
                    for oo in range(2):
                        oc4 = o2 * 2 + oo
                        for i in range(8):
                            MM(bk_[:, oo * 256:oo * 256 + ntok], sv[:, i, oc4 * 128:(oc4 + 1) * 128], ZT[:, i, 0:ntok],
                               R=[slot.b, ZT.b], W=[bk_.b], start=i == 0, stop=i == 7)
                    oc = 4 * b + o2 * 2
                    TT(DVE, xT[:, oc:oc + 2, 0:ntok], b2(bk_, ntok), xT[:, oc:oc + 2, 0:ntok], ALU.add,
                       R=[bk_.b, xT.b], W=[xT.b])

        def ffn(ntok, l):
            rstd = rows[0]
            rstd_of_x(ntok, Bz[4], rstd)
            hf = Bz[1]
            apply_norm(ntok, rstd, V_NF0 if l == 0 else V_NF1, hf)
            for j in range(11):
                slot = wload(f"f{l}i{j}")
                sv = slot.t[:, :].rearrange("p (k g c) -> p k g c", k=8, g=2)
                for jj in range(2):
                    hc = 2 * j + jj
                    bk_ = nb()
                    for g in range(2):
                        for kc in range(8):
                            MM(bk_[:, g * 256:g * 256 + ntok], sv[:, kc, g, jj * 128:(jj + 1) * 128], hf[:, kc, 0:ntok],
                               R=[slot.b, hf.b], W=[bk_.b], start=kc == 0, stop=kc == 7)
                    sgt = A[2 + hc % 2]
                    ACTV(sgt[:, 0, 0:ntok], bk_[:, 0:ntok], AF.Silu, R=[bk_.b], W=[sgt.b], relax=hc >= 2)
                    TT(DVE, hid_ap[:, hc, 0:ntok], bk_[:, 256:256 + ntok], sgt[:, 0, 0:ntok], ALU.mult,
                       R=[bk_.b, sgt.b], W=hid_bufs, relax=hc > 0)
            for o2 in range(4):
                bk_ = nb()
                for oo in range(2):
                    oc = o2 * 2 + oo
                    slot = wload(f"f{l}o{oc}")
                    sv = slot.t[:, 0:NJ * 128].rearrange("p (j c) -> p j c", j=NJ)
                    for jc in range(NJ):
                        MM(bk_[:, oo * 256:oo * 256 + ntok], sv[:, jc, :], hid_ap[:, jc, 0:ntok],
                           R=[slot.b] + hid_bufs, W=[bk_.b], start=jc == 0, stop=jc == NJ - 1)
                TT(DVE, xT[:, o2 * 2:o2 * 2 + 2, 0:ntok], b2(bk_, ntok), xT[:, o2 * 2:o2 * 2 + 2, 0:ntok], ALU.add,
                   R=[bk_.b] + ([xT.b] if o2 == 0 else []), W=[xT.b], relax=o2 > 0)

        def rope16(src16_tb, src_ap16, nh, ntok, out_write):
            pass

        def shared_kv(ntok, rstd, rope_src):
            nblk = (ntok + 127) // 128
            hkv = Bz[1]
            apply_norm(ntok, rstd, V_KVN, hkv)
            LOAD(ropet[:, :, 0:ntok], rope_src, d_rope, W=[ropet.b])
            slot = wload("wkv")
            sv = slot.t[:, :].rearrange("p (k c) -> p k c", k=8)
            for g2 in range(2):
                bk_ = nb()
                for gg in range(2):
                    g = g2 * 2 + gg
                    for kc in range(8):
                        MM(bk_[0:64, gg * 256:gg * 256 + ntok], sv[:, kc, g * 64:(g + 1) * 64], hkv[:, kc, 0:ntok],
                           R=[slot.b, hkv.b], W=[bk_.b], start=kc == 0, stop=kc == 7)
                    ACTV(Kf[:, g, 0:ntok], bk_[0:64, gg * 256:gg * 256 + ntok], AF.Identity, R=[bk_.b, bk.b],
                         W=[Kf.b], bias=bk[:, g:g + 1])
            for blk in range(nblk):
                tb = min(128, ntok - blk * 128)
                bk_ = nb()
                for kc in range(8):
                    MM(bk_[0:tb, 0:256], hkv[:, kc, blk * 128:blk * 128 + tb], sv[:, kc, 256:512], R=[slot.b, hkv.b],
                       W=[bk_.b], start=kc == 0, stop=kc == 7)
                TT(DVE, Vt[0:tb, 1 + blk, :], bk_[0:tb, 0:256], bvb[0:tb, :], ALU.add, R=[bk_.b, bvb.b], W=[Vt.b])
            CP(DVE, K16[:, :, 0:ntok], Kf[0:16, :, 0:ntok], R=[Kf.b], W=[K16.b])
            for g2 in range(2):
                bk_ = nb()
                for gg in range(2):
                    g = g2 * 2 + gg
                    MM(bk_[0:16, gg * 256:gg * 256 + ntok], p16_b, K16[:, g, 0:ntok], R=[cb.b, K16.b], W=[bk_.b])
                t1 = A[3].t[0:16, 0:2, 0:ntok]
                t2 = A[3].t[0:16, 2:4, 0:ntok]
                cosb = bc(ropet[:, 0, 0:ntok].unsqueeze(1), [16, 2, ntok])
                sinb = bc(ropet[:, 1, 0:ntok].unsqueeze(1), [16, 2, ntok])
                TT(DVE, t1, Kf[0:16, g2 * 2:g2 * 2 + 2, 0:ntok], cosb, ALU.mult, R=[Kf.b, ropet.b], W=[r16.b])
                TT(DVE, t2, bk_[0:16, :].rearrange("p (a n) -> p a n", a=2)[:, :, 0:ntok], sinb, ALU.mult,
                   R=[bk_.b, ropet.b], W=[r16.b])
                TT(DVE, Kf[0:16, g2 * 2:g2 * 2 + 2, 0:ntok], t1, t2, ALU.add, R=[r16.b], W=[Kf.b])
            CP(ACT, KF[:, :, 128:128 + ntok], Kf[:, :, 0:ntok], R=[Kf.b], W=[KF.b])

        def roll_window(ntok):
            CP(POOL, KF[:, :, 0:128], KF[:, :, ntok:ntok + 128], R=[KF.b], W=[KF.b])
            CP(POOL, Vt[:, 0, :], Vt[:, ntok // 128, :], R=[Vt.b], W=[Vt.b])

        def attention(ntok, rstd, mask_first, mask_rest):
            nblk = (ntok + 127) // 128
            hq = Bz[0]
            apply_norm(ntok, rstd, V_NM1, hq)
            Qb = Bz[2:4]
            def qv(h):
                return Qb[h // 8][0:64, h % 8, 0:ntok]
            for b in range(2):
                slot = wload(f"wq{b}")
                sv = slot.t[:, :].rearrange("p (k c) -> p k c", k=8)
                for h2 in range(4):
                    bk_ = nb()
                    for hh in range(2):
                        hl = h2 * 2 + hh
                        for kc in range(8):
                            MM(bk_[0:64, hh * 256:hh * 256 + ntok], sv[:, kc, hl * 64:(hl + 1) * 64], hq[:, kc, 0:ntok],
                               R=[slot.b, hq.b], W=[bk_.b], start=kc == 0, stop=kc == 7)
                        h = b * 8 + hl
                        ACTV(qv(h), bk_[0:64, hh * 256:hh * 256 + ntok], AF.Identity, R=[bk_.b, bq.b],
                             W=[Qb[b].b], bias=bq[:, h:h + 1], relax=hh > 0)
                    h0 = b * 8 + h2 * 2
                    bk2 = nb()
                    for hh in range(2):
                        MM(bk2[0:16, hh * 256:hh * 256 + ntok], p16_b, Qb[b][0:16, (h0 + hh) % 8, 0:ntok],
                           R=[cb.b, Qb[b].b], W=[bk2.b])
                    t1 = A[3].t[0:16, 0:2, 0:ntok]
                    t2 = A[3].t[0:16, 2:4, 0:ntok]
                    cosb = bc(ropet[:, 0, 0:ntok].unsqueeze(1), [16, 2, ntok])
                    sinb = bc(ropet[:, 1, 0:ntok].unsqueeze(1), [16, 2, ntok])
                    qs = Qb[b][0:16, h0 % 8:h0 % 8 + 2, 0:ntok]
                    TT(DVE, t1, qs, cosb, ALU.mult, R=[Qb[b].b, ropet.b], W=[r16.b])
                    TT(DVE, t2, bk2[0:16, :].rearrange("p (a n) -> p a n", a=2)[:, :, 0:ntok], sinb, ALU.mult,
                       R=[bk2.b, ropet.b], W=[r16.b])
                    TT(DVE, qs, t1, t2, ALU.add, R=[r16.b], W=[Qb[b].b])
            OT = Bz[1]
            units = [(qb, g) for qb in range(nblk) for g in range(4)]
            grp = {}

            def front(idx):
                qb, g = units[idx]
                tb = min(128, ntok - qb * 128)
                nk = 128 + tb
                mk = mask_first if qb == 0 else mask_rest
                u = idx % 3
                sm = asm[u]
                E = Eb[u]
                bkS = [nb(), nb()]
                for j in range(4):
                    h = 4 * g + j
                    o_ = bkS[j // 2][0:tb, (j % 2) * 256:(j % 2) * 256 + nk]
                    MM(o_, Qb[h // 8][0:64, h % 8, qb * 128:qb * 128 + tb], KF[0:64, g, qb * 128:qb * 128 + nk],
                       R=[Qb[h // 8].b, KF.b], W=[bkS[j // 2].b], start=True, stop=(mk == 2))
                    if mk != 2:
                        MM(o_, mkb[0:4, 0:tb], mkb[0:4, 128 + mk * 256:128 + mk * 256 + nk], R=[mkb.b],
                           W=[bkS[j // 2].b], start=False, stop=True)
                for jj in range(2):
                    RED(sm[0:tb, 0, 2 * jj:2 * jj + 2], bkS[jj][0:tb, :].rearrange("p (a n) -> p a n", a=2)[:, :, 0:nk],
                        ALU.max, R=[bkS[jj].b], W=[sm.b], relax=jj > 0)
                sk = sinkb[0:tb, 4 * g:4 * g + 4]
                STT(sm[0:tb, 0, :], sm[0:tb, 0, :], ATT_SCALE, sk, ALU.mult, ALU.max, R=[sm.b, sinkb.b], W=[sm.b])
                TS(DVE, sm[0:tb, 1, :], sm[0:tb, 0, :], -1.0, ALU.mult, R=[sm.b], W=[sm.b])
                TT(DVE, sm[0:tb, 2, :], sk, sm[0:tb, 0, :], ALU.subtract, R=[sm.b, sinkb.b], W=[sm.b])
                sa = asa[u]
                for j in range(4):
                    ACTV(E[0:tb, j, 0:nk], bkS[j // 2][0:tb, (j % 2) * 256:(j % 2) * 256 + nk], AF.Exp,
                         R=[bkS[j // 2].b, sm.b], W=[E.b, sa.b], bias=sm[0:tb, 1, j:j + 1], scale=ATT_SCALE,
                         accum=sa[0:tb, 1, j:j + 1], relax=j > 0)
                ACTV(sa[0:tb, 0, :], sm[0:tb, 2, :], AF.Exp, R=[sm.b], W=[sa.b])

            def back_norm(idx):
                qb, g = units[idx]
                tb = min(128, ntok - qb * 128)
                nk = 128 + tb
                u = idx % 3
                sm = asm[u]
                E = Eb[u]
                sa = asa[u]
                TT(DVE, sm[0:tb, 4, :], sa[0:tb, 1, :], sa[0:tb, 0, :], ALU.add, R=[sa.b], W=[sm.b], relax=True)
                RECIP(sm[0:tb, 5, :], sm[0:tb, 4, :], R=[sm.b], W=[sm.b])
                TT(DVE, E[0:tb, :, 0:nk], E[0:tb, :, 0:nk], bc(sm[0:tb, 5, :].unsqueeze(2), [tb, 4, nk]), ALU.mult,
                   R=[E.b, sm.b], W=[E.b])

            def back_tr(idx):
                qb, g = units[idx]
                tb = min(128, ntok - qb * 128)
                E = Eb[idx % 3]
                bkT = nb()
                tv = bfv(bkT).rearrange("p (a b n) -> p a b n", a=4, b=2)
                for j in range(4):
                    TR(tv[0:128, j, 0, 0:tb], E[0:tb, j, 0:128], cb[0:tb, 0:tb], R=[E.b, cb.b], W=[bkT.b])
                    TR(tv[0:tb, j, 1, 0:tb], E[0:tb, j, 128:128 + tb], cb[0:tb, 0:tb], R=[E.b, cb.b], W=[bkT.b])
                pT = pTb[idx % 2]
                CP(ACT, pT[0:128, :, 0, 0:tb], tv[0:128, :, 0, 0:tb], R=[bkT.b], W=[pT.b])
                CP(ACT, pT[0:tb, :, 1, 0:tb], tv[0:tb, :, 1, 0:tb], R=[bkT.b], W=[pT.b], relax=True)

            def back_pv(idx):
                qb, g = units[idx]
                tb = min(128, ntok - qb * 128)
                pT = pTb[idx % 2]
                if g % 2 == 0:
                    grp["bkO"] = nb()
                    held.append(grp["bkO"])
                bkO = grp["bkO"]
                for j in range(4):
                    hh = j % 2
                    slot = (2 * g + j // 2) % 4
                    ps = slice(hh * 64, (hh + 1) * 64)
                    MM(bkO[ps, slot * 128:slot * 128 + tb], Vt[0:128, qb, g * 64:(g + 1) * 64], pT[0:128, j, 0, 0:tb],
                       R=[Vt.b, pT.b], W=[bkO.b], start=True, stop=False)
                    MM(bkO[ps, slot * 128:slot * 128 + tb], Vt[0:tb, qb + 1, g * 64:(g + 1) * 64],
                       pT[0:tb, j, 1, 0:tb], R=[Vt.b, pT.b], W=[bkO.b], start=False, stop=True)
                if g % 2 == 1:
                    p4 = g // 2
                    CP(ACT, OT[:, p4 * 4:p4 * 4 + 4, qb * 128:qb * 128 + tb],
                       bkO[:, :].rearrange("p (a n) -> p a n", a=4)[:, :, 0:tb], R=[bkO.b], W=[OT.b])
                    held.remove(bkO)

            nu = len(units)
            for it in range(nu + 2):
                if 1 <= it <= nu:
                    back_norm(it - 1)
                if it < nu:
                    front(it)
                if 1 <= it <= nu:
                    back_tr(it - 1)
                if it >= 2:
                    back_pv(it - 2)
            dump(f"OT{ntok}_{bank_state['i']}", OT, [128, 8, NT], BF16)
            for b in range(2):
                slot = wload(f"wp{b}")
                sv = slot.t[:, :].rearrange("p (k c) -> p k c", k=8)
                for o2 in range(2):
                    bk_ = nb()
                    for oo in range(2):
                        oc4 = o2 * 2 + oo
                        for kc in range(8):
                            MM(bk_[:, oo * 256:oo * 256 + ntok], sv[:, kc, oc4 * 128:(oc4 + 1) * 128],
                               OT[:, kc, 0:ntok], R=[slot.b, OT.b], W=[bk_.b], start=kc == 0, stop=kc == 7)
                    for oo in range(2):
                        oc = 4 * b + o2 * 2 + oo
                        STT(xT[:, oc, 0:ntok], bk_[:, oo * 256:oo * 256 + ntok], vecs[:, V_BO, oc:oc + 1],
                            xT[:, oc, 0:ntok], ALU.add, ALU.add, R=[bk_.b, vecs.b, xT.b], W=[xT.b])

        def final_out(ntok, ydst):
            nblk = (ntok + 127) // 128
            rstd = rows[0]
            rstd_of_x(ntok, Bz[4], rstd)
            yf = A[0]
            apply_norm(ntok, rstd, V_NFIN, yf)
            yo = A[1:3]
            for blk in range(nblk):
                tb = min(128, ntok - blk * 128)
                yov = yo[blk].t[:, :, :].rearrange("p a n -> p (a n)")
                for k4 in range(2):
                    bk_ = nb()
                    for kk_ in range(4):
                        kc = k4 * 4 + kk_
                        TR(bk_[0:tb, kk_ * 128:(kk_ + 1) * 128], yf[:, kc, blk * 128:blk * 128 + tb], ident_f,
                           R=[yf.b, cst.b], W=[bk_.b])
                    CP(ACT if k4 == 0 else DVE, yov[0:tb, k4 * 512:(k4 + 1) * 512], bk_[0:tb, :], R=[bk_.b],
                       W=[yo[blk].b])
                STORE(ydst[blk * 128:blk * 128 + tb, :], yov[0:tb, 0:1024], R=[yo[blk].b], ds=d_outb[blk])

        def out_state(st_dst, sh_dst):
            STORE(st_dst, ST.t[:, :, :], R=[ST.b], ds=d_st)
            STORE(sh_dst, hprev.t[:, :], R=[hprev.b], ds=d_st)

        def out_cache(ck_dst, cv_dst, ntok):
            nk = min(ntok, 128)
            bk_ = nb()
            tv = bfv(bk_)
            for g in range(4):
                TR(tv[0:nk, g * 64:(g + 1) * 64], KF[0:64, g, 128 + ntok - nk:128 + ntok], cb[0:64, 0:64],
                   R=[KF.b, cb.b], W=[bk_.b])
            CP(DVE, kvo[0:nk, 0, :], tv[0:nk, 0:256], R=[bk_.b], W=[kvo.b])
            lastblk = (ntok + 127) // 128
            CP(DVE, kvo[0:nk, 1, :], Vt[0:nk, lastblk, :], R=[Vt.b], W=[kvo.b])
            STORE(ck_dst[128 - nk:128, :], kvo[0:nk, 0, :], R=[kvo.b], ds=d_st)
            STORE(cv_dst[128 - nk:128, :], kvo[0:nk, 1, :], R=[kvo.b], ds=d_st)

        stg = {"n": 0}

        def stage():
            PHASES.append(P_pe.n)
            stg["n"] += 1
            if _STOP is not None and stg["n"] >= _STOP:
                raise StopBuild()

        def run_tile(xsrc, ntok, mode, rope_src=None, ydst=None, mask_first=0, mask_rest=0):
            nch = ntok // 64
            full = mode != "warm"
            stage()
            load_x(xsrc, ntok)
            flush_stores()
            stage()
            bonus, G, Vfm = time_mix(ntok, mode)
            stage()
            Ystore = A[4]
            wkv_all(nch, Vfm, full, Ystore)
            stage()
            if not full:
                return
            tm_out(ntok, Ystore, bonus, G)
            stage()
            ffn(ntok, 0)
            stage()
            rstd = rows[0]
            rstd_of_x(ntok, Bz[4], rstd)
            shared_kv(ntok, rstd, rope_src)
            stage()
            if mode == "l0":
                roll_window(ntok)
                return
            attention(ntok, rstd, mask_first, mask_rest)
            stage()
            ffn(ntok, 1)
            stage()
            final_out(ntok, ydst)
            stage()

        try:
          for t in range(n_warm):
              last = t == n_warm - 1
              if n_warm > 1:
                  pc_emit(len(pc_queue) if t >= n_warm - 2 else pc_per_tile)
              run_tile(xw[t * NT:(t + 1) * NT, :], NT, "l0" if last else "warm", rope_src=ropew)
          for t in range(n_main):
              run_tile(xm[t * NT:(t + 1) * NT, :], NT, "full", rope_src=ropem[:, :, t * NT:(t + 1) * NT],
                       ydst=y_m[t * NT:(t + 1) * NT, :], mask_first=1 if t == 0 else 0, mask_rest=0)
              if t < n_main - 1:
                  roll_window(NT)
          out_state(st_m, sh_m)
          out_cache(ck_m, cv_m, NT)
          flush_stores()
          LOAD(ST.t[:, :, :], st0, d_const, W=[ST.b])
          LOAD(hprev.t[:, :], sh0, d_const, W=[hprev.b])
          LOAD(kvo[:, 0, :], ck, d_const, W=[kvo.b])
          LOAD(kvo[:, 1, :], cv, d_const, W=[kvo.b])
          for dst in (ST, hprev, kvo):
              dst.b.w = (d_const, d_const.n)
          CP(ACT, STb.t[:, :, :], ST.t[:, :, :], R=[ST.b], W=[STb.b])
          CP(DVE, Vt[:, 0, :], kvo[:, 1, :], R=[kvo.b], W=[Vt.b])
          bk_ = nb()
          for g in range(4):
              TR(bk_[0:64, g * 128:(g + 1) * 128], kvo[:, 0, g * 64:(g + 1) * 64], ident_f, R=[kvo.b, cst.b], W=[bk_.b])
          CP(ACT, KF[:, :, 0:128], bk_[0:64, :].rearrange("p (a n) -> p a n", a=4), R=[bk_.b], W=[KF.b])
          S.dma(SP, lambda e: e.dma_start(out=ck_s[0:64, :], in_=ck[64:128, :]), d_st, R=(), W=())
          S.dma(SP, lambda e: e.dma_start(out=cv_s[0:64, :], in_=cv[64:128, :]), d_st, R=(), W=())
          run_tile(xs, 64, "full", rope_src=ropes, ydst=y_s, mask_first=2, mask_rest=2)
          out_state(st_s, sh_s)
          out_cache(ck_s, cv_s, 64)
          flush_stores()
        except StopBuild:
            flush_stores()
        for dsm in (d_out, d_st, d_outb[0], d_outb[1]):
            if dsm.n:
                SP.ops.append({"k": "w", "p": dsm, "v": dsm.n})

        Sched.finalize(prods)
        with nc.Block() as block:
            @block.tensor
            def _(e):
                Sched.replay(e, PE)

            @block.scalar
            def _(e):
                Sched.replay(e, ACT)

            @block.vector
            def _(e):
                Sched.replay(e, DVE)

            @block.gpsimd
            def _(e):
                Sched.replay(e, POOL)

            @block.sync
            def _(e):
                Sched.replay(e, SP)
    global LAST_NOPS
    LAST_NOPS = {E.name: len(E.ops) for E in (PE, ACT, DVE, POOL, SP)}
    return nc


def _nat(v):
    return np.ascontiguousarray(np.asarray(v, np.float32).reshape(8, 128).T)


def _pf(v):
    return np.ascontiguousarray(np.asarray(v, np.float32).reshape(2, 8, 64).transpose(0, 2, 1).reshape(128, 8))


def _consts():
    c = np.zeros((128, CS_W), np.float32)
    p = np.arange(128)
    c[:, CS_ID:CS_ID + 128] = np.eye(128)
    c[:, CS_BLK:CS_BLK + 128] = (p[:, None] // 64 == p[None, :] // 64)
    j = (p % 64)[:, None]
    t = np.arange(64)[None, :]
    ms = (j < t).astype(np.float32)
    mi = (j <= t).astype(np.float32)
    mst = (t < j).astype(np.float32)
    i64 = (j == t).astype(np.float32)
    c[:, CS_MS:CS_MS + 64] = ms
    c[:, CS_MI:CS_MI + 64] = mi
    c[:, CS_MST:CS_MST + 64] = mst
    c[:, CS_I64:CS_I64 + 64] = i64
    sel = np.zeros((128, 2, 64), np.float32)
    sel[:64, 0] = 1
    sel[64:, 1] = 1
    c[:, CS_MSBD:CS_MSBD + 128] = (sel * ms[:, None, :]).reshape(128, 128)
    c[:, CS_MIBD:CS_MIBD + 128] = (sel * mi[:, None, :]).reshape(128, 128)
    c[:, CS_MSTBD:CS_MSTBD + 128] = (sel * mst[:, None, :]).reshape(128, 128)
    c[:, CS_SELBD:CS_SELBD + 128] = sel.reshape(128, 128)
    for i in range(8):
        c[i + 8, CS_P16 + i] = -1.0
        c[i, CS_P16 + i + 8] = 1.0
    c[:, CS_RESET:CS_RESET + NT] = (np.arange(NT) % 64 != 0)[None, :]
    return c


def _rope_tab(pos):
    half = 8
    inv = np.power(np.float32(500000.0), -np.arange(half, dtype=np.float32) * np.float32(2.0 / 16)).astype(np.float32)
    ang = pos.astype(np.float32)[:, None] * inv[None, :]
    cos = np.cos(ang).astype(np.float32).T
    sin = np.sin(ang).astype(np.float32).T
    out = np.zeros((16, 2, len(pos)), np.float32)
    out[0:8, 0] = cos
    out[8:16, 0] = cos
    out[0:8, 1] = sin
    out[8:16, 1] = sin
    return out


_NC_CACHE = {}
LAST_NOPS = None
_HOOK = None
PHASES = []
_DBG = False
_STOP = None


class StopBuild(Exception):
    pass


def _run(inputs, n_warm, n_main, seq_half, past_len, dbg=False):
    f = lambda k: np.asarray(inputs[k], np.float32)
    x_prompt, x_sample = f("x_prompt"), f("x_sample")
    nb_p = x_prompt.shape[0]
    ncores = 2 * nb_p
    dbg = dbg or _DBG
    key = (n_warm, n_main, dbg)
    if key not in _NC_CACHE:
        _NC_CACHE[key] = build(n_warm, n_main, dbg)
    nc = _NC_CACHE[key]
    NW, NM = n_warm * NT, n_main * NT
    assert NM == seq_half and NW == seq_half
    vec_list = [None] * NVEC
    vec_list[V_NM0] = _nat(f("norm_mix")[0]); vec_list[V_NF0] = _nat(f("norm_ffn")[0])
    vec_list[V_NM1] = _nat(f("norm_mix")[1]); vec_list[V_NF1] = _nat(f("norm_ffn")[1])
    vec_list[V_KVN] = _nat(f("kv_norm")); vec_list[V_NFIN] = _nat(f("norm_final"))
    for j in range(6):
        vec_list[V_MU0 + j] = _nat(f("rw_mu")[0, j])
    vec_list[V_BO] = _nat(f("b_o")[0])
    vec_list[V_W0] = _pf(f("rw_w0")[0]); vec_list[V_A0] = _pf(f("rw_a0")[0])
    vec_list[V_KK] = _pf(f("rw_k_k")[0]); vec_list[V_KA] = _pf(f("rw_k_a")[0])
    vec_list[V_C1] = None
    vec_list[V_RK] = _pf(f("rw_r_k")[0].reshape(-1))
    vec_list[V_LNW] = _pf(f("rw_lnx_w")[0]); vec_list[V_LNB] = _pf(f("rw_lnx_b")[0])
    vec_list[V_C1] = np.zeros((128, 8), np.float32)
    vecs = np.ascontiguousarray(np.stack(vec_list, axis=1))
    cst = _consts()
    bq = np.ascontiguousarray(f("b_q")[0].reshape(16, 64).T)
    bkk = np.ascontiguousarray(f("b_kv")[0:256].reshape(4, 64).T)
    bvb = np.ascontiguousarray(np.broadcast_to(f("b_kv")[256:512][None, :], (128, 256)))
    sinkb = np.ascontiguousarray(np.broadcast_to(f("attn_sinks")[0].reshape(1, 16), (128, 16)))
    masku = np.zeros((4, 128), np.float32)
    masku[0, 0:64] = 1.0
    masku[1, 64:128] = 1.0
    masku[2, :] = 1.0
    shared = {
        "vecs": vecs, "cst": cst, "bq": bq, "bk": bkk, "bvb": bvb, "sinkb": sinkb,
        "rw_w_rkv": np.ascontiguousarray(f("rw_w_rkv")[0]), "rw_w1": np.ascontiguousarray(f("rw_w1")[0]),
        "rw_w2": np.ascontiguousarray(f("rw_w2")[0]), "rw_a1": np.ascontiguousarray(f("rw_a1")[0]),
        "rw_a2": np.ascontiguousarray(f("rw_a2")[0]), "rw_g1": np.ascontiguousarray(f("rw_g1")[0]),
        "rw_g2": np.ascontiguousarray(f("rw_g2")[0]), "rw_w_o": np.ascontiguousarray(f("rw_w_o")[0]),
        "w_kv": f("w_kv"), "w_q": np.ascontiguousarray(f("w_q")[0]), "w_o": np.ascontiguousarray(f("w_o")[0]),
        "ffn_w_in": f("ffn_w_in"), "ffn_w_out": f("ffn_w_out"),
        "ropes": _rope_tab(past_len + np.arange(64)),
    }
    in_maps = []
    state_wkv, state_shift = f("state_wkv")[0], f("state_shift")[0]
    cache_k, cache_v = f("cache_k"), f("cache_v")
    for c in range(ncores):
        b, half = c // 2, c % 2
        m = dict(shared)
        if half == 0:
            m["xw"] = np.zeros((NW, D), np.float32)
            m["xm"] = np.ascontiguousarray(x_prompt[b, 0:NM])
            pos_m = np.arange(NM)
            pos_w = np.arange(NT)
            first_masked = True
        else:
            m["xw"] = np.ascontiguousarray(x_prompt[b, 0:NW])
            m["xm"] = np.ascontiguousarray(x_prompt[b, NW:NW + NM])
            pos_m = NW + np.arange(NM)
            pos_w = NW - NT + np.arange(NT)
            first_masked = False
        mv = np.zeros((4, 2, 256), np.float32)
        mv[0, :, 192:256] = NEG
        mv[1, :, 0:64] = NEG
        if first_masked:
            mv[2, 1, 0:128] = NEG
        m["masku"] = masku
        m["maskv"] = mv
        m["ropem"] = _rope_tab(pos_m)
        m["ropew"] = _rope_tab(pos_w)
        sb = c % x_sample.shape[0]
        m["xs"] = np.ascontiguousarray(x_sample[sb])
        s = state_wkv[sb]
        m["st0"] = np.ascontiguousarray(s.reshape(2, 8, 64, 64).transpose(0, 3, 1, 2).reshape(128, 8, 64))
        m["sh0"] = _nat(state_shift[sb])
        m["ck"] = np.ascontiguousarray(cache_k[sb].reshape(128, 256))
        m["cv"] = np.ascontiguousarray(cache_v[sb].reshape(128, 256))
        in_maps.append(m)
    if _HOOK is not None:
        R = _HOOK(nc, in_maps)
    else:
        res = run_bass_kernel_spmd(nc, in_maps, core_ids=list(range(ncores)))
        R = res.results

    def unst(a):
        return np.ascontiguousarray(np.asarray(a).reshape(2, 64, 8, 64).transpose(0, 2, 3, 1).reshape(16, 64, 64))

    def unsh(a):
        return np.ascontiguousarray(np.asarray(a).T.reshape(-1))

    y_prompt = np.stack([np.concatenate([R[2 * b]["y_m"], R[2 * b + 1]["y_m"]], axis=0) for b in range(nb_p)])
    nsb = x_sample.shape[0]
    y_sample = np.stack([R[b]["y_s"] for b in range(nsb)])
    wkv_p = np.stack([unst(R[2 * b + 1]["st_m"]) for b in range(nb_p)])[None]
    shift_p = np.stack([unsh(R[2 * b + 1]["sh_m"]) for b in range(nb_p)])[None]
    k_p = np.stack([np.asarray(R[2 * b + 1]["ck_m"]).reshape(128, 4, 64) for b in range(nb_p)])
    v_p = np.stack([np.asarray(R[2 * b + 1]["cv_m"]).reshape(128, 4, 64) for b in range(nb_p)])
    wkv_s = np.stack([unst(R[b]["st_s"]) for b in range(nsb)])[None]
    shift_s = np.stack([unsh(R[b]["sh_s"]) for b in range(nsb)])[None]
    k_s = np.stack([np.asarray(R[b]["ck_s"]).reshape(128, 4, 64) for b in range(nsb)])
    v_s = np.stack([np.asarray(R[b]["cv_s"]).reshape(128, 4, 64) for b in range(nsb)])
    outs = (y_prompt, y_sample, wkv_p, shift_p, k_p, v_p, wkv_s, shift_s, k_s, v_s)
    outs = tuple(np.ascontiguousarray(o, dtype=np.float32) for o in outs)
    return outs


def kernel(**inputs):
    seq = np.asarray(inputs["x_prompt"]).shape[1]
    half = seq // 2
    return _run(inputs, half // NT, half // NT, half, 1024)
```

```python
import contextlib
import numpy as np
import concourse.bass as bass
import concourse.mybir as mybir
from concourse.bass_utils import run_bass_kernel_spmd

F32 = mybir.dt.float32
BF16 = mybir.dt.bfloat16
AF = mybir.ActivationFunctionType
ALU = mybir.AluOpType
AX = mybir.AxisListType

D = 1024
NT = 256
DFF = 2816
NJ = DFF // 128
C0 = 0.6065306597126334
RMS_EPS = 1e-5
LNX_EPS = 64e-5
ATT_SCALE = 0.125
NEG = -1e30

(V_NM0, V_NF0, V_NM1, V_NF1, V_KVN, V_NFIN, V_MU0, V_MU1, V_MU2, V_MU3, V_MU4, V_MU5, V_BO,
 V_W0, V_A0, V_KK, V_KA, V_C1, V_RK, V_LNW, V_LNB) = range(21)
NVEC = 21

CS_ID = 0
CS_BLK = 128
CS_MS = 256
CS_MI = 320
CS_MST = 384
CS_I64 = 448
CS_MSBD = 512
CS_MIBD = 640
CS_MSTBD = 768
CS_SELBD = 896
CS_P16 = 1024
CS_RESET = 1040
CS_W = CS_RESET + NT


class Buf:
    __slots__ = ("name", "w", "r", "psum")

    def __init__(self, name):
        self.name = name
        self.w = None
        self.r = {}
        self.psum = False


class Prod:
    def __init__(self, name, semh, is_dma=False):
        self.name = name
        self.semh = semh
        self.is_dma = is_dma
        self.n = 0
        self.recs = []
        self.kcs = []


class Eng:
    def __init__(self, name, prod):
        self.name = name
        self.prod = prod
        self.seen = {}
        self.ops = []


class Sched:
    def _waits(self, E, reads, writes, relax=False):
        need = {}

        def req(ev):
            if ev is None:
                return
            p, v = ev
            if need.get(p, 0) < v:
                need[p] = v

        for b in reads:
            req(b.w)
            if b.psum:
                for p, v in b.r.items():
                    if p is not E.prod:
                        req((p, v))
        for b in writes:
            if b.w is not None and not (relax and b.w[0] is E.prod):
                req(b.w)
            for p, v in b.r.items():
                if not (relax and p is E.prod):
                    req((p, v))
        for p, v in need.items():
            if p is E.prod and E.name == "pe":
                continue
            if E.seen.get(p, 0) >= v:
                continue
            E.seen[p] = v
            kc = p.kcs[v - 1]
            if kc:
                for q, w in kc.items():
                    if E.seen.get(q, 0) < w:
                        E.seen[q] = w
            if not p.is_dma:
                p.recs[v - 1]["inc"] = True
            E.ops.append({"k": "w", "p": p, "v": v})

    @staticmethod
    def _mark(ev, reads, writes):
        p, v = ev
        for b in reads:
            if b.r.get(p, 0) < v:
                b.r[p] = v
        for b in writes:
            b.w = ev
            b.r = {}

    def op(self, E, fn, R=(), W=(), relax=False):
        self._waits(E, R, W, relax)
        p = E.prod
        p.n += 1
        rec = {"k": "o", "fn": fn, "inc": False, "p": p}
        p.recs.append(rec)
        p.kcs.append(dict(E.seen))
        E.ops.append(rec)
        self._mark((p, p.n), R, W)
        return rec

    def dma(self, Q, fn, dsem, R=(), W=()):
        self._waits(Q, R, W)
        dsem.n += 1
        dsem.kcs.append(dict(Q.seen))
        rec = {"k": "d", "fn": fn, "p": dsem}
        Q.ops.append(rec)
        self._mark((dsem, dsem.n), R, W)

    @staticmethod
    def finalize(prods):
        for p in prods:
            if p.is_dma:
                continue
            c = 0
            p.vals = []
            for r in p.recs:
                if r["inc"]:
                    c += 1
                p.vals.append(c)

    @staticmethod
    def replay(e, E):
        for o in E.ops:
            if o["k"] == "w":
                p = o["p"]
                val = 16 * o["v"] if p.is_dma else p.vals[o["v"] - 1]
                e.wait_ge(p.semh, val)
            elif o["k"] == "d":
                o["fn"](e).then_inc(o["p"].semh, 16)
            else:
                ins = o["fn"](e)
                if o["inc"]:
                    ins.then_inc(o["p"].semh, 1)


class TB:
    def __init__(self, t, name):
        self.t = t
        self.b = Buf(name)

    def __getitem__(self, idx):
        return self.t[idx]


def bc(ap, shape):
    return ap.broadcast_to(list(shape))


def build(n_warm, n_main, dbg=False):
    nc = bass.Bass("TRN2", target_bir_lowering=False)
    S = Sched()
    es = contextlib.ExitStack()

    def din(name, shape, dt=F32):
        return nc.dram_tensor(name, list(shape), dt, kind="ExternalInput").ap()

    def dout(name, shape, dt=F32):
        return nc.dram_tensor(name, list(shape), dt, kind="ExternalOutput").ap()

    NW, NM = n_warm * NT, n_main * NT
    xw = din("xw", [max(NW, 1), D])
    xm = din("xm", [NM, D])
    xs = din("xs", [64, D])
    st0 = din("st0", [128, 8, 64])
    sh0 = din("sh0", [128, 8])
    ck = din("ck", [128, 256])
    cv = din("cv", [128, 256])
    masku_d = din("masku", [4, 128])
    maskv_d = din("maskv", [4, 2, 256])
    ropew = din("ropew", [16, 2, NT])
    ropem = din("ropem", [16, 2, NM])
    ropes = din("ropes", [16, 2, 64])
    vecs_d = din("vecs", [128, NVEC, 8])
    cst_d = din("cst", [128, CS_W])
    bq_d = din("bq", [64, 16])
    bk_d = din("bk", [64, 4])
    bvb_d = din("bvb", [128, 256])
    sink_d = din("sinkb", [128, 16])
    w_rkv = din("rw_w_rkv", [3, D, D])
    w_1 = din("rw_w1", [D, 64])
    w_2 = din("rw_w2", [64, D])
    a_1 = din("rw_a1", [D, 64])
    a_2 = din("rw_a2", [64, D])
    g_1 = din("rw_g1", [D, 160])
    g_2 = din("rw_g2", [160, D])
    w_o0 = din("rw_w_o", [D, D])
    w_kv = din("w_kv", [D, 512])
    w_q = din("w_q", [D, D])
    w_o1 = din("w_o", [D, D])
    f_in = din("ffn_w_in", [2, D, 2 * DFF])
    f_out = din("ffn_w_out", [2, DFF, D])

    y_m = dout("y_m", [NM, D])
    y_s = dout("y_s", [64, D])
    st_m = dout("st_m", [128, 8, 64])
    sh_m = dout("sh_m", [128, 8])
    ck_m = dout("ck_m", [128, 256])
    cv_m = dout("cv_m", [128, 256])
    st_s = dout("st_s", [128, 8, 64])
    sh_s = dout("sh_s", [128, 8])
    ck_s = dout("ck_s", [128, 256])
    cv_s = dout("cv_s", [128, 256])
    dbg_outs = {}

    SLOT = 4096
    blocks = {}
    order = []

    def defblock(name, E):
        blocks[name] = {"E": E, "idx": len(order)}
        order.append(name)

    for nm in ("wk", "wv", "wr"):
        defblock(nm + "0", 4096)
        defblock(nm + "1", 4096)
    defblock("wo0", 4096)
    defblock("wo1", 4096)
    for l in range(2):
        for j in range(11):
            defblock(f"f{l}i{j}", 4096)
        for oc in range(8):
            defblock(f"f{l}o{oc}", NJ * 128)
    defblock("wkv", 4096)
    defblock("wq0", 4096)
    defblock("wq1", 4096)
    defblock("wp0", 4096)
    defblock("wp1", 4096)
    nblk_w = len(order)
    wsc = nc.dram_tensor("wsc", [nblk_w, 128, SLOT], BF16, kind="Internal").ap()
    wsc_buf = Buf("wsc")

    with es:
        def sem(name):
            return es.enter_context(nc.semaphore(name))

        def sbt(name, shape, dt=F32):
            return TB(es.enter_context(nc.sbuf_tensor("s_" + name, list(shape), dt)), name)

        P_pe = Prod("pe", sem("s_pe"))
        P_act = Prod("act", sem("s_act"))
        P_dve = Prod("dve", sem("s_dve"))
        P_pool = Prod("pool", sem("s_pool"))
        PE = Eng("pe", P_pe)
        ACT = Eng("act", P_act)
        DVE = Eng("dve", P_dve)
        POOL = Eng("pool", P_pool)
        SP = Eng("sp", None)
        prods = [P_pe, P_act, P_dve, P_pool]

        def dsem(name):
            p = Prod(name, sem(name), is_dma=True)
            prods.append(p)
            return p

        xT = sbt("xT", [128, 8, NT])
        NRING = 4
        ring = [sbt(f"ring{i}", [128, SLOT], BF16) for i in range(NRING)]
        ring_sem = [dsem(f"d_ring{i}") for i in range(NRING)]
        ring_state = {"i": 0}
        w1s = sbt("w1s", [128, 8, 64], BF16)
        a1s = sbt("a1s", [128, 8, 64], BF16)
        g1s = sbt("g1s", [128, 8, 160], BF16)
        w2a2 = sbt("w2a2", [128, D], BF16)
        g2s = sbt("g2s", [128, 2, D], BF16)
        vecs = sbt("vecs", [128, NVEC, 8])
        cst = sbt("cst", [128, CS_W])
        cb = sbt("cb", [128, 128 * 3 + 64 + 16], BF16)
        mkb = sbt("mkb", [4, 128 + 512], BF16)
        bq = sbt("bq", [64, 16])
        bk = sbt("bk", [64, 4])
        bvb = sbt("bvb", [128, 256])
        sinkb = sbt("sinkb", [128, 16])
        ST = sbt("ST", [128, 8, 64])
        STb = sbt("STb", [128, 8, 64], BF16)
        hprev = sbt("hprev", [128, 8])
        KF = sbt("KF", [64, 4, 128 + NT], BF16)
        Vt = sbt("Vt", [128, 1 + NT // 128, 256], BF16)
        Abig = es.enter_context(nc.sbuf_tensor("s_Abig", [128, 5, 8, NT], F32))
        A = [TB(Abig[:, i], f"A{i}") for i in range(5)]
        Bz = [sbt(f"B{i}", [128, 8, NT], BF16) for i in range(7)]
        masku = TB(A[0].t[0:4, 0, 0:128], "masku")
        masku.b = A[0].b
        maskv = TB(A[0].t[0:4, 1:3, :], "maskv")
        maskv.b = A[0].b
        xin = TB(A[4].t[:, :, :].rearrange("p (b a) n -> p b (a n)", b=2), "xin")
        xin.b = A[4].b
        ropet = TB(A[3].t[0:16, 6:8, :], "ropet")
        ropet.b = A[3].b
        hid_ap = Abig[:, 0:2].rearrange("p a b n -> p (a b n)").bitcast(BF16)[:, 0:NJ * NT].rearrange(
            "p (j n) -> p j n", n=NT)
        hid_bufs = [A[0].b, A[1].b]
        AR = sbt("AR", [128, 8, NT // 64, 2, 64], BF16)
        KBD = sbt("KBD", [128, 8, NT // 64, 2, 64], BF16)
        BBD = sbt("BBD", [128, 8, NT // 64, 2, 64], BF16)
        rows = [sbt("row0", [128, NT])]
        gC = sbt("gC", [128, 8, NT // 64])
        t1ab = sbt("t1ab", [128, NT], BF16)
        tgb = sbt("tgb", [128, 2, NT], BF16)
        KtmBD = [sbt(f"KtmBD{i}", [128, 8, 128], BF16) for i in range(2)]
        BtmBD = [sbt(f"BtmBD{i}", [128, 8, 128], BF16) for i in range(2)]
        Vtm = [sbt(f"Vtm{i}", [128, 8, 64], BF16) for i in range(2)]
        Aak = [sbt(f"Aak{i}", [128, 8, 64], BF16) for i in range(2)]
        Akr = [sbt(f"Akr{i}", [128, 8, 64], BF16) for i in range(2)]
        Abr = [sbt(f"Abr{i}", [128, 8, 64], BF16) for i in range(2)]
        TBD = [sbt(f"TBD{i}", [128, 8, 128], BF16) for i in range(2)]
        TBk = [sbt(f"TBk{i}", [128, 8, 128], BF16) for i in range(2)]
        PBD = [sbt(f"PBD{i}", [128, 8, 128], BF16) for i in range(2)]
        PTBD = [sbt(f"PTBD{i}", [128, 8, 128], BF16) for i in range(2)]
        X0b = sbt("X0b", [128, 8, 64], BF16)
        Xb = sbt("Xb", [128, 8, 64], BF16)
        Stmp = TB(A[3].t[:, 0:2, :].rearrange("p a (b v) -> p (a b) v", v=64), "Stmp")
        Stmp.b = A[3].b
        gns = sbt("gns", [128, 6, 8 * (NT // 64)])
        Eb = [sbt(f"Eb{i}", [128, 4, 256], BF16) for i in range(3)]
        pTb = [sbt(f"pTb{i}", [128, 4, 2, 128], BF16) for i in range(2)]
        asm = [sbt(f"asm{i}", [128, 6, 4]) for i in range(3)]
        asa = [sbt(f"asa{i}", [128, 2, 4]) for i in range(3)]
        Kf = TB(A[2].t[0:64, 0:4, :], "Kf")
        Kf.b = A[2].b
        K16 = TB(Bz[4].t[0:16, 0:4, :], "K16")
        K16.b = Bz[4].b
        r16 = TB(A[3].t[0:16, 0:6, :].rearrange("p (k a) n -> p k (a n)", k=3), "r16")
        r16.b = A[3].b
        kvo = TB(A[4].t[:, 4:6, :], "kvo")
        kvo.b = A[4].b
        banks = [TB(es.enter_context(nc.psum_tensor(f"bank{i}", [128, 512], F32)), f"bank{i}") for i in range(8)]
        for b_ in banks:
            b_.b.psum = True
        bank_state = {"i": 0}

        d_const = dsem("d_const")
        d_pre = dsem("d_pre")
        d_x = dsem("d_x")
        d_rope = dsem("d_rope")
        d_out = dsem("d_out")
        d_outb = [dsem("d_out0"), dsem("d_out1")]
        d_st = dsem("d_st")

        held = []

        def nb():
            while True:
                b = banks[bank_state["i"] % 8]
                bank_state["i"] += 1
                if b not in held:
                    return b

        def MM(out, lhsT, rhs, R, W, start=True, stop=True):
            S.op(PE, lambda e: e.matmul(out, lhsT, rhs, start=start, stop=stop), R=R, W=W)

        def TR(out, in_, ident, R, W):
            S.op(PE, lambda e: e.transpose(out, in_, ident), R=R, W=W)

        def ACTV(out, in_, func, R, W, bias=None, scale=None, accum=None, relax=False):
            kw = {}
            if bias is not None:
                kw["bias"] = bias
            if scale is not None:
                kw["scale"] = scale
            if accum is not None:
                kw["accum_out"] = accum
            S.op(ACT, lambda e: e.activation(out=out, in_=in_, func=func, **kw), R=R, W=W, relax=relax)

        def TT(E, out, in0, in1, op, R, W, relax=False):
            S.op(E, lambda e: e.tensor_tensor(out=out, in0=in0, in1=in1, op=op), R=R, W=W, relax=relax)

        def TS(E, out, in0, s1, op0, R, W, s2=None, op1=None, relax=False):
            if op1 is None:
                S.op(E, lambda e: e.tensor_scalar(out=out, in0=in0, scalar1=s1, scalar2=None, op0=op0), R=R, W=W,
                     relax=relax)
            else:
                S.op(E, lambda e: e.tensor_scalar(out=out, in0=in0, scalar1=s1, scalar2=s2, op0=op0, op1=op1), R=R, W=W,
                     relax=relax)

        def STT(out, in0, scalar, in1, op0, op1, R, W, relax=False):
            S.op(DVE, lambda e: e.scalar_tensor_tensor(out=out, in0=in0, scalar=scalar, in1=in1, op0=op0, op1=op1),
                 R=R, W=W, relax=relax)

        def CP(E, out, in_, R, W, relax=False):
            if E is ACT:
                S.op(E, lambda e: e.activation(out=out, in_=in_, func=AF.Copy), R=R, W=W, relax=relax)
            else:
                S.op(E, lambda e: e.tensor_copy(out=out, in_=in_), R=R, W=W, relax=relax)

        def RECIP(out, in_, R, W):
            S.op(DVE, lambda e: e.reciprocal(out=out, in_=in_), R=R, W=W)

        def RED(out, in_, op, R, W, relax=False):
            S.op(DVE, lambda e: e.tensor_reduce(out=out, in_=in_, axis=AX.X, op=op), R=R, W=W, relax=relax)

        def MSET(E, ap, val, W):
            S.op(E, lambda e: e.memset(ap, val), R=(), W=W)

        def LOAD(out, in_, ds, W, R=()):
            S.dma(SP, lambda e: e.dma_start(out=out, in_=in_), ds, R=R, W=W)

        pending_stores = []

        def STORE(out, in_, R, ds=None):
            pending_stores.append((out, in_, list(R), ds or d_out))

        def flush_stores():
            for out, in_, R, ds in pending_stores:
                S.dma(SP, (lambda o, i: (lambda e: e.dma_start(out=o, in_=i)))(out, in_), ds, R=R, W=())
            for out, in_, R, ds in pending_stores:
                for b in R:
                    b.r[ds] = ds.n
            pending_stores.clear()

        def dump(name, tb, shape, dt=F32):
            if not dbg:
                return
            o = dout("dbg_" + name, shape, dt)
            dbg_outs[name] = o
            S.dma(SP, lambda e: e.dma_start(out=o, in_=tb.t[:]), d_out, R=[tb.b], W=())

        def cbv(off, n, rows=128):
            return cb[0:rows, off:off + n]

        ident_b = cb[:, 0:128]
        blk_b = cb[:, 128:256]
        ones_b = cb[:, 256:384]
        i64_b = cb[:, 384:448]
        p16_b = cb[0:16, 448:464]
        ident_f = cst[:, CS_ID:CS_ID + 128]

        for (dst, src) in ((vecs, vecs_d), (cst, cst_d), (masku, masku_d), (maskv, maskv_d), (bq, bq_d), (bk, bk_d),
                           (bvb, bvb_d), (sinkb, sink_d)):
            LOAD(dst.t[:], src, d_const, W=[dst.b])
        for dst in (vecs, cst, masku, maskv, bq, bk, bvb, sinkb):
            dst.b.w = (d_const, d_const.n)
        CP(DVE, mkb[:, 0:128], masku.t, R=[masku.b], W=[mkb.b])
        CP(DVE, mkb[:, 128:640].rearrange("p (a n) -> p a n", a=2), maskv.t, R=[maskv.b], W=[mkb.b])
        TS(DVE, vecs[:, V_C1, :], vecs[:, V_KA, :], -1.0, ALU.mult, R=[vecs.b], W=[vecs.b], s2=1.0, op1=ALU.add)

        def PCAST(out, in_, W=()):
            S.dma(POOL, lambda e: e.dma_start(out=out, in_=in_), d_pre, R=(), W=list(W))

        PCAST(w1s.t[:], w_1.rearrange("(k p) c -> p k c", p=128), [w1s.b])
        PCAST(a1s.t[:], a_1.rearrange("(k p) c -> p k c", p=128), [a1s.b])
        PCAST(g1s.t[:], g_1.rearrange("(k p) c -> p k c", p=128), [g1s.b])
        for h in range(2):
            PCAST(w2a2.t[0:64, :].rearrange("p (i h d) -> p i h d", i=8, h=2)[:, :, h, :],
                  w_2.rearrange("p (h i d) -> p h i d", h=2, i=8)[:, h], [w2a2.b])
            PCAST(w2a2.t[64:128, :].rearrange("p (i h d) -> p i h d", i=8, h=2)[:, :, h, :],
                  a_2.rearrange("p (h i d) -> p h i d", h=2, i=8)[:, h], [w2a2.b])
            PCAST(g2s.t[:, 0, :].rearrange("p (i h d) -> p i h d", i=8, h=2)[:, :, h, :],
                  g_2[0:128, :].rearrange("p (h i d) -> p h i d", h=2, i=8)[:, h], [g2s.b])
            PCAST(g2s.t[0:32, 1, :].rearrange("p (i h d) -> p i h d", i=8, h=2)[:, :, h, :],
                  g_2[128:160, :].rearrange("p (h i d) -> p h i d", h=2, i=8)[:, h], [g2s.b])

        def scr(name):
            return wsc[blocks[name]["idx"]]

        pre_sems = [dsem(f"d_prq{i}") for i in range(8)]
        pc_state = {"j": 0, "hist": []}
        pc_queue = []
        for nm_ in order:
            blocks[nm_]["bufs"] = []

        def QCAST(name, out, in_):
            pc_queue.append((name, out, in_))

        def pc_emit(n):
            while n > 0 and pc_queue:
                n -= 1
                name, out, in_ = pc_queue.pop(0)
                j = pc_state["j"]
                pc_state["j"] += 1
                sm = pre_sems[j % 8]
                if j >= 4:
                    ps_, pv_ = pc_state["hist"][j - 4]
                    POOL.ops.append({"k": "w", "p": ps_, "v": pv_})
                b_ = Buf(f"pc{j}")
                S.dma(POOL, (lambda o, i: (lambda e: e.dma_start(out=o, in_=i)))(out, in_), sm, R=(), W=[b_])
                pc_state["hist"].append((sm, sm.n))
                blocks[name]["bufs"].append(b_)

        for j, nm in ((1, "wk"), (2, "wv"), (0, "wr")):
            src = w_rkv[j].rearrange("(k p) (h c) -> p k h c", p=128, h=2)
            for b in range(2):
                dstv = scr(nm + str(b)).rearrange("p (k i h d) -> p k i h d", k=8, i=4, h=2)
                for i4 in range(4):
                    for h in range(2):
                        QCAST(nm + str(b), dstv[:, :, i4, h, :], src[:, :, h, (4 * b + i4) * 64:(4 * b + i4 + 1) * 64])
        n_first = len(pc_queue) - 16
        src = w_o0.rearrange("(h i d) c -> h d i c", h=2, i=8)
        for b in range(2):
            for h in range(2):
                QCAST(f"wo{b}", scr(f"wo{b}")[h * 64:(h + 1) * 64, :].rearrange("p (i c) -> p i c", i=8),
                      src[h, :, :, b * 512:(b + 1) * 512])

        def q_ffn(l):
            src = f_in[l].rearrange("(k p) (g c) -> p k g c", p=128, g=2)
            for j in range(11):
                for g in range(2):
                    QCAST(f"f{l}i{j}", scr(f"f{l}i{j}").rearrange("p (k g c) -> p k g c", k=8, g=2)[:, :, g, :],
                          src[:, :, g, j * 256:(j + 1) * 256])
            src = f_out[l].rearrange("(j p) c -> p j c", p=128)
            for oc in range(8):
                QCAST(f"f{l}o{oc}", scr(f"f{l}o{oc}")[:, 0:NJ * 128].rearrange("p (j c) -> p j c", j=NJ),
                      src[:, :, oc * 128:(oc + 1) * 128])

        q_ffn(0)
        QCAST("wkv", scr("wkv").rearrange("p (k c) -> p k c", k=8), w_kv.rearrange("(k p) c -> p k c", p=128))
        for b in range(2):
            QCAST(f"wq{b}", scr(f"wq{b}").rearrange("p (k c) -> p k c", k=8),
                  w_q.rearrange("(k p) c -> p k c", p=128)[:, :, b * 512:(b + 1) * 512])
        for b in range(2):
            QCAST(f"wp{b}", scr(f"wp{b}").rearrange("p (k c) -> p k c", k=8),
                  w_o1.rearrange("(k p) c -> p k c", p=128)[:, :, b * 512:(b + 1) * 512])
        q_ffn(1)
        pc_emit(n_first if n_warm > 1 else len(pc_queue))
        pc_per_tile = (len(pc_queue) + max(n_warm - 2, 1) - 1) // max(n_warm - 2, 1) if n_warm > 1 else 0

        for dst in (w1s, a1s, g1s, w2a2, g2s):
            dst.b.w = (d_pre, d_pre.n)
        CP(DVE, cb[:, 0:128], cst[:, CS_ID:CS_ID + 128], R=[cst.b], W=[cb.b])
        CP(DVE, cb[:, 128:256], cst[:, CS_BLK:CS_BLK + 128], R=[cst.b], W=[cb.b])
        MSET(DVE, cb[:, 256:384], 1.0, W=[cb.b])
        CP(DVE, cb[:, 384:448], cst[:, CS_I64:CS_I64 + 64], R=[cst.b], W=[cb.b])
        CP(DVE, cb[0:16, 448:464], cst[0:16, CS_P16:CS_P16 + 16], R=[cst.b], W=[cb.b])
        MSET(POOL, KBD.t[:], 0.0, W=[KBD.b])
        MSET(POOL, BBD.t[:], 0.0, W=[BBD.b])
        MSET(POOL, ST.t[:], 0.0, W=[ST.b])
        MSET(POOL, STb.t[:], 0.0, W=[STb.b])
        MSET(POOL, hprev.t[:], 0.0, W=[hprev.b])
        MSET(POOL, KF.t[:], 0.0, W=[KF.b])
        MSET(POOL, Vt.t[:], 0.0, W=[Vt.b])
        MSET(POOL, tgb.t[:], 0.0, W=[tgb.b])

        def wload(name):
            i = ring_state["i"] % NRING
            ring_state["i"] += 1
            E = blocks[name]["E"]
            assert blocks[name]["bufs"], name
            LOAD(ring[i].t[:, 0:E], scr(name)[:, 0:E], ring_sem[i], W=[ring[i].b], R=blocks[name]["bufs"])
            return ring[i]

        def load_x(src, ntok):
            nblk = (ntok + 127) // 128
            for blk in range(nblk):
                tb = min(128, ntok - blk * 128)
                LOAD(xin[0:tb, blk, :], src[blk * 128:blk * 128 + tb, :], d_x, W=[xin.b])
            for kc in range(0, 8, 2):
                bk_ = nb()
                for k2 in range(2):
                    for blk in range(nblk):
                        tb = min(128, ntok - blk * 128)
                        TR(bk_[:, k2 * 256 + blk * 128:k2 * 256 + blk * 128 + tb],
                           xin[0:tb, blk, (kc + k2) * 128:(kc + k2 + 1) * 128], ident_f[0:tb, 0:tb],
                           R=[xin.b, cst.b], W=[bk_.b])
                CP(ACT, xT[:, kc:kc + 2, 0:ntok], bk_[:, :].rearrange("p (a n) -> p a n", a=2)[:, :, 0:ntok],
                   R=[bk_.b], W=[xT.b], relax=kc > 0)

        def rstd_of_x(ntok, sqbuf, rstd):
            ACTV(sqbuf[:, :, 0:ntok], xT[:, :, 0:ntok], AF.Square, R=[xT.b], W=[sqbuf.b])
            bk_ = nb()
            for kc in range(8):
                MM(bk_[:, 0:ntok], ones_b, sqbuf[:, kc, 0:ntok], R=[cb.b, sqbuf.b], W=[bk_.b], start=kc == 0,
                   stop=kc == 7)
            TS(DVE, rstd[:, 0:ntok], bk_[:, 0:ntok], 1.0 / D, ALU.mult, R=[bk_.b], W=[rstd.b], s2=RMS_EPS,
               op1=ALU.add)
            ACTV(rstd[:, 0:ntok], rstd[:, 0:ntok], AF.Sqrt, R=[rstd.b], W=[rstd.b])
            RECIP(rstd[:, 0:ntok], rstd[:, 0:ntok], R=[rstd.b], W=[rstd.b])

        def apply_norm(ntok, rstd, vi, out):
            for kc in range(8):
                STT(out[:, kc, 0:ntok], xT[:, kc, 0:ntok], vecs[:, vi, kc:kc + 1], rstd[:, 0:ntok], ALU.mult,
                    ALU.mult, R=[xT.b, vecs.b, rstd.b], W=[out.b], relax=kc > 0)

        def proj8(ntok, blkname, rhs_tb, evac):
            for b in range(2):
                slot = wload(blkname + str(b))
                sv = slot.t[:, :].rearrange("p (k c) -> p k c", k=8)
                for i2 in range(2):
                    bk_ = nb()
                    for ii in range(2):
                        i4 = i2 * 2 + ii
                        for kc in range(8):
                            MM(bk_[:, ii * 256:ii * 256 + ntok], sv[:, kc, i4 * 128:(i4 + 1) * 128],
                               rhs_tb[:, kc, 0:ntok], R=[slot.b, rhs_tb.b], W=[bk_.b], start=kc == 0, stop=kc == 7)
                    evac(bk_, 4 * b + i2 * 2)

        def b2(bk_, ntok):
            return bk_[:, :].rearrange("p (a n) -> p a n", a=2)[:, :, 0:ntok]

        def time_mix(ntok, mode):
            nch = ntok // 64
            full = mode != "warm"
            hbuf, xx, kf, sg, av = A
            xl = Bz[0:2]
            Vfm = Bz[2]
            G = Bz[3]
            rf = Bz[5]
            kk = Bz[6]
            rstd = rows[0]
            rstd_of_x(ntok, Bz[4], rstd)
            apply_norm(ntok, rstd, V_NM0, hbuf)
            TT(DVE, xx[:, :, 1:ntok], hbuf[:, :, 0:ntok - 1], hbuf[:, :, 1:ntok], ALU.subtract, R=[hbuf.b],
               W=[xx.b])
            TT(DVE, xx[:, :, 0:1], hprev[:, :].unsqueeze(2), hbuf[:, :, 0:1], ALU.subtract, R=[hbuf.b, hprev.b],
               W=[xx.b])
            CP(ACT, hprev[:, :].unsqueeze(2), hbuf[:, :, ntok - 1:ntok], R=[hbuf.b], W=[hprev.b])
            stage()
            st = {"i": 0}

            def lerp(j):
                o = xl[st["i"] % 2]
                st["i"] += 1
                for kc in range(8):
                    STT(o[:, kc, 0:ntok], xx[:, kc, 0:ntok], vecs[:, V_MU0 + j, kc:kc + 1], hbuf[:, kc, 0:ntok],
                        ALU.mult, ALU.add, R=[xx.b, vecs.b, hbuf.b], W=[o.b], relax=kc > 0)
                return o

            o = lerp(2)
            proj8(ntok, "wk", o, lambda bk_, i0: CP(ACT, kf[:, i0:i0 + 2, 0:ntok], b2(bk_, ntok), R=[bk_.b],
                                                     W=[kf.b], relax=i0 > 0))
            stage()
            o = lerp(3)
            proj8(ntok, "wv", o, lambda bk_, i0: CP(ACT, Vfm[:, i0:i0 + 2, 0:ntok], b2(bk_, ntok), R=[bk_.b],
                                                     W=[Vfm.b], relax=i0 > 0))
            if full:
                o = lerp(0)
                proj8(ntok, "wr", o, lambda bk_, i0: CP(ACT, rf[:, i0:i0 + 2, 0:ntok], b2(bk_, ntok), R=[bk_.b],
                                                         W=[rf.b], relax=i0 > 0))
            stage()
            o = lerp(1)
            bk_ = nb()
            for kc in range(8):
                MM(bk_[0:64, 0:ntok], w1s[:, kc, :], o[:, kc, 0:ntok], R=[w1s.b, o.b], W=[bk_.b], start=kc == 0,
                   stop=kc == 7)
            ACTV(t1ab[0:64, 0:ntok], bk_[0:64, 0:ntok], AF.Tanh, R=[bk_.b], W=[t1ab.b])
            o = lerp(4)
            bk_ = nb()
            for kc in range(8):
                MM(bk_[64:128, 0:ntok], a1s[:, kc, :], o[:, kc, 0:ntok], R=[a1s.b, o.b], W=[bk_.b], start=kc == 0,
                   stop=kc == 7)
            CP(ACT, t1ab[64:128, 0:ntok], bk_[64:128, 0:ntok], R=[bk_.b], W=[t1ab.b])
            if full:
                o = lerp(5)
                bk_ = nb()
                for kc in range(8):
                    MM(bk_[:, 0:ntok], g1s[:, kc, 0:128], o[:, kc, 0:ntok], R=[g1s.b, o.b], W=[bk_.b],
                       start=kc == 0, stop=kc == 7)
                for kc in range(8):
                    MM(bk_[0:32, 256:256 + ntok], g1s[:, kc, 128:160], o[:, kc, 0:ntok], R=[g1s.b, o.b],
                       W=[bk_.b], start=kc == 0, stop=kc == 7)
                ACTV(tgb[:, 0, 0:ntok], bk_[:, 0:ntok], AF.Sigmoid, R=[bk_.b], W=[tgb.b])
                ACTV(tgb[0:32, 1, 0:ntok], bk_[0:32, 256:256 + ntok], AF.Sigmoid, R=[bk_.b], W=[tgb.b])
            stage()
            for i in range(0, 8, 2):
                bkW = nb()
                bkA_ = nb()
                for ii in range(2):
                    cs_ = slice((i + ii) * 128, (i + ii + 1) * 128)
                    MM(bkW[:, ii * 256:ii * 256 + ntok], w2a2[0:64, cs_], t1ab[0:64, 0:ntok], R=[w2a2.b, t1ab.b],
                       W=[bkW.b])
                    MM(bkA_[:, ii * 256:ii * 256 + ntok], w2a2[64:128, cs_], t1ab[64:128, 0:ntok],
                       R=[w2a2.b, t1ab.b], W=[bkA_.b])
                for ii in range(2):
                    ACTV(sg[:, i + ii, 0:ntok], bkW[:, ii * 256:ii * 256 + ntok], AF.Sigmoid, R=[bkW.b, vecs.b],
                         W=[sg.b], bias=vecs[:, V_W0, i + ii:i + ii + 1], relax=(i + ii) > 0)
                    ACTV(av[:, i + ii, 0:ntok], bkA_[:, ii * 256:ii * 256 + ntok], AF.Sigmoid, R=[bkA_.b, vecs.b],
                         W=[av.b], bias=vecs[:, V_A0, i + ii:i + ii + 1], relax=(i + ii) > 0)
            if full:
                for i in range(8):
                    cs_ = slice(i * 128, (i + 1) * 128)
                    bk2 = nb()
                    MM(bk2[:, 0:ntok], g2s[:, 0, cs_], tgb[:, 0, 0:ntok], R=[g2s.b, tgb.b], W=[bk2.b], start=True,
                       stop=False)
                    MM(bk2[:, 0:ntok], g2s[0:32, 1, cs_], tgb[0:32, 1, 0:ntok], R=[g2s.b, tgb.b], W=[bk2.b],
                       start=False, stop=True)
                    CP(ACT, G[:, i, 0:ntok], bk2[:, 0:ntok], R=[bk2.b], W=[G.b], relax=i > 0)
            stage()
            sqk = Bz[4]
            for i in range(8):
                ACTV(sqk[:, i, 0:ntok], kf[:, i, 0:ntok], AF.Square, R=[kf.b, vecs.b], W=[sqk.b],
                     scale=vecs[:, V_KK, i:i + 1], relax=i > 0)
            sdk = xx
            for i in range(0, 8, 2):
                bk_ = nb()
                for ii in range(2):
                    MM(bk_[:, ii * 256:ii * 256 + ntok], blk_b, sqk[:, i + ii, 0:ntok], R=[cb.b, sqk.b], W=[bk_.b])
                TS(DVE, sdk[:, i:i + 2, 0:ntok], b2(bk_, ntok), 1e-24, ALU.max, R=[bk_.b], W=[sdk.b], relax=i > 0)
            ACTV(sdk[:, :, 0:ntok], sdk[:, :, 0:ntok], AF.Ln, R=[sdk.b], W=[sdk.b], scale=float(2.0 ** 40))
            ACTV(sdk[:, :, 0:ntok], sdk[:, :, 0:ntok], AF.Exp, R=[sdk.b], W=[sdk.b], scale=-0.5,
                 bias=float(20.0 * np.log(2.0)))
            for i in range(8):
                STT(kk[:, i, 0:ntok], kf[:, i, 0:ntok], vecs[:, V_KK, i:i + 1], sdk[:, i, 0:ntok], ALU.mult, ALU.mult,
                    R=[kf.b, vecs.b, sdk.b], W=[kk.b], relax=i > 0)
            stage()
            cs = hbuf
            for i in range(8):
                S.op(DVE, (lambda i: lambda e: e.tensor_tensor_scan(
                    out=cs[:, i, 0:ntok], data0=cst[:, CS_RESET:CS_RESET + ntok], data1=sg[:, i, 0:ntok],
                    initial=0.0, op0=ALU.mult, op1=ALU.add))(i), R=[cst.b, sg.b], W=[cs.b], relax=i > 0)
            stage()
            gam = sg
            gami = xx
            ACTV(gam[:, :, 0:ntok], cs[:, :, 0:ntok], AF.Exp, R=[cs.b], W=[gam.b], scale=-C0)
            ACTV(gami[:, :, 0:ntok], cs[:, :, 0:ntok], AF.Exp, R=[cs.b], W=[gami.b], scale=C0)
            CP(ACT, gC[:, :, 0:nch], gam[:, :, 63:ntok:64], R=[gam.b], W=[gC.b])
            stage()
            u = hbuf
            for i in range(8):
                ACTV(u[:, i, 0:ntok], av[:, i, 0:ntok], AF.Identity, R=[av.b, vecs.b, gam.b, gami.b], W=[u.b],
                     scale=vecs[:, V_KA, i:i + 1], bias=vecs[:, V_C1, i:i + 1], relax=i > 0)
            TT(DVE, kf[:, :, 0:ntok], kf[:, :, 0:ntok], u[:, :, 0:ntok], ALU.mult, R=[kf.b, u.b], W=[kf.b])
            TT(DVE, av[:, :, 0:ntok], kk[:, :, 0:ntok], av[:, :, 0:ntok], ALU.mult, R=[kk.b, av.b], W=[av.b])
            if full:
                rk = u
                TT(DVE, rk[:, :, 0:ntok], rf[:, :, 0:ntok], bc(vecs[:, V_RK, :].unsqueeze(2), [128, 8, ntok]),
                   ALU.mult, R=[rf.b, vecs.b, u.b], W=[rk.b])
                rkb = Bz[4]
                TT(DVE, rkb[:, :, 0:ntok], rk[:, :, 0:ntok], kf[:, :, 0:ntok], ALU.mult, R=[rk.b, kf.b], W=[rkb.b])
                bonus = hbuf
                for i in range(0, 8, 2):
                    bk_ = nb()
                    for ii in range(2):
                        MM(bk_[:, ii * 256:ii * 256 + ntok], blk_b, rkb[:, i + ii, 0:ntok], R=[cb.b, rkb.b],
                           W=[bk_.b])
                    TT(DVE, bonus[:, i:i + 2, 0:ntok], b2(bk_, ntok), Vfm[:, i:i + 2, 0:ntok], ALU.mult,
                       R=[bk_.b, Vfm.b, rk.b], W=[bonus.b], relax=i > 0)
                TT(DVE, AR[:, :, 0:nch, 1, :], rf[:, :, 0:ntok].rearrange("p a (c t) -> p a c t", t=64),
                   gam[:, :, 0:ntok].rearrange("p a (c t) -> p a c t", t=64), ALU.mult, R=[rf.b, gam.b], W=[AR.b])
            stage()
            kkv = kk[:, :, 0:ntok].rearrange("p a (c t) -> p a c t", t=64)
            gmv = gam[:, :, 0:ntok].rearrange("p a (c t) -> p a c t", t=64)
            arv = AR[:, :, 0:nch, 0, :]
            STT(arv[:, :, :, 1:64], kkv[:, :, :, 1:64], -1.0, gmv[:, :, :, 0:63], ALU.mult, ALU.mult,
                R=[kk.b, gam.b], W=[AR.b])
            TS(DVE, arv[:, :, :, 0:1], kkv[:, :, :, 0:1], -1.0, ALU.mult, R=[kk.b], W=[AR.b])
            stage()
            kpv = kf[:, :, 0:ntok].rearrange("p a (c t) -> p a c t", t=64)
            bv = av[:, :, 0:ntok].rearrange("p a (c t) -> p a c t", t=64)
            giv = gami[:, :, 0:ntok].rearrange("p a (c t) -> p a c t", t=64)
            for h in range(2):
                ps = slice(h * 64, (h + 1) * 64)
                TT(DVE, KBD[ps, :, 0:nch, h, :], kpv[ps], giv[ps], ALU.mult, R=[kf.b, gami.b], W=[KBD.b])
                TT(DVE, BBD[ps, :, 0:nch, h, :], bv[ps], giv[ps], ALU.mult, R=[av.b, gami.b], W=[BBD.b])
            return (hbuf if full else None), G, Vfm

        def bfv(bk_):
            return bk_[:, :].bitcast(BF16)

        def wkv_part1(c, par, Vfm, full):
            Kt, Bt, Vm = KtmBD[par], BtmBD[par], Vtm[par]
            cs_ = slice(c * 64, (c + 1) * 64)
            bkK = nb()
            bkB = nb()
            for i in range(8):
                TR(bfv(bkK)[:, i * 128:(i + 1) * 128], KBD[:, i, c, :, :].rearrange("p h t -> p (h t)"), ident_b,
                   R=[KBD.b, cb.b], W=[bkK.b])
            CP(ACT, Kt.t[:, :, :].rearrange("p a b -> p (a b)"), bfv(bkK), R=[bkK.b], W=[Kt.b])
            for i in range(8):
                TR(bfv(bkB)[:, i * 128:(i + 1) * 128], BBD[:, i, c, :, :].rearrange("p h t -> p (h t)"), ident_b,
                   R=[BBD.b, cb.b], W=[bkB.b])
            CP(DVE, Bt.t[:, :, :].rearrange("p a b -> p (a b)"), bfv(bkB), R=[bkB.b], W=[Bt.b])
            yield
            bkV = [nb(), nb()]
            for i in range(8):
                for h in range(2):
                    ps = slice(h * 64, (h + 1) * 64)
                    TR(bfv(bkV[h])[ps, i * 64:(i + 1) * 64], Vfm[ps, i, cs_], cb[ps, h * 64:(h + 1) * 64],
                       R=[Vfm.b, cb.b], W=[bkV[h].b])
            for h in range(2):
                ps = slice(h * 64, (h + 1) * 64)
                CP(ACT, Vm.t[ps, :, :].rearrange("p a b -> p (a b)"), bfv(bkV[h])[ps, 0:512],
                   R=[bkV[h].b], W=[Vm.b], relax=h > 0)
            yield
            ncol = 128 if full else 64
            msbd = cst[:, CS_MSBD:CS_MSBD + 128].rearrange("p (h t) -> p h t", h=2)
            mstbd = cst[:, CS_MSTBD:CS_MSTBD + 128].rearrange("p (h t) -> p h t", h=2)
            ms = cst[:, CS_MS:CS_MS + 64]
            mi = cst[:, CS_MI:CS_MI + 64]

            def bdsrc(bk_, off):
                v = bk_[:, :].rearrange("p (a n) -> p a n", a=4)[:, :, off:off + 64]
                return bc(v.unsqueeze(2), [128, 4, 2, 64])

            def bdmask(m):
                return bc(m.unsqueeze(1), [128, 4, 2, 64])

            for g4 in range(2):
                bkA = nb()
                for ii in range(4):
                    i = g4 * 4 + ii
                    rhs = AR[:, i, c, 0:(2 if full else 1), :].rearrange("p a t -> p (a t)")
                    MM(bkA[:, ii * 128:ii * 128 + ncol], KBD[:, i, c, :, :].rearrange("p h t -> p (h t)"), rhs,
                       R=[KBD.b, AR.b], W=[bkA.b])
                o4 = slice(g4 * 4, g4 * 4 + 4)
                vA = bkA[:, :].rearrange("p (a n) -> p a n", a=4)
                TT(DVE, Aak[par].t[:, o4, :], vA[:, :, 0:64], bc(ms.unsqueeze(1), [128, 4, 64]), ALU.mult,
                   R=[bkA.b, cst.b], W=[Aak[par].b])
                if full:
                    TT(DVE, Akr[par].t[:, o4, :], vA[:, :, 64:128], bc(mi.unsqueeze(1), [128, 4, 64]), ALU.mult,
                       R=[bkA.b, cst.b], W=[Akr[par].b])
                bkB2 = nb()
                for ii in range(4):
                    i = g4 * 4 + ii
                    rhs = AR[:, i, c, 0:(2 if full else 1), :].rearrange("p a t -> p (a t)")
                    MM(bkB2[:, ii * 128:ii * 128 + ncol], BBD[:, i, c, :, :].rearrange("p h t -> p (h t)"), rhs,
                       R=[BBD.b, AR.b], W=[bkB2.b])
                if full:
                    TT(DVE, Abr[par].t[:, o4, :], bkB2[:, :].rearrange("p (a n) -> p a n", a=4)[:, :, 64:128],
                       bc(mi.unsqueeze(1), [128, 4, 64]), ALU.mult, R=[bkB2.b, cst.b], W=[Abr[par].b])
                TT(DVE, PBD[0].t[:, o4, :].rearrange("p a (h t) -> p a h t", h=2), bdsrc(bkB2, 0), bdmask(msbd),
                   ALU.mult, R=[bkB2.b, cst.b], W=[PBD[0].b])
                yield
            bkN = [nb(), nb()]
            for i in range(8):
                for h in range(2):
                    ps = slice(h * 64, (h + 1) * 64)
                    MM(bkN[h][ps, i * 64:(i + 1) * 64], AR[ps, i, c, 0, :], BBD[ps, i, c, h, :], R=[AR.b, BBD.b],
                       W=[bkN[h].b])
            for h in range(2):
                ps = slice(h * 64, (h + 1) * 64)
                vN = bkN[h][ps, :].rearrange("p (a n) -> p a n", a=8)
                TT(DVE, PTBD[0].t[ps, :, :].rearrange("p a (h t) -> p a h t", h=2),
                   bc(vN.unsqueeze(2), [64, 8, 2, 64]), bc(mstbd[ps].unsqueeze(1), [64, 8, 2, 64]), ALU.mult,
                   R=[bkN[h].b, cst.b], W=[PTBD[0].b])
            yield
            for k in range(6):
                cur, nxt = k % 2, (k + 1) % 2
                for g4 in range(2):
                    o4 = slice(g4 * 4, g4 * 4 + 4)
                    if k >= 1:
                        bkT_ = nb()
                        for ii in range(4):
                            i = g4 * 4 + ii
                            MM(bkT_[:, ii * 128:(ii + 1) * 128], PTBD[cur].t[:, i, :], TBk[cur].t[:, i, :],
                               R=[PTBD[cur].b, TBk[cur].b], W=[bkT_.b])
                    if k <= 4:
                        bkPT = nb()
                        for ii in range(4):
                            i = g4 * 4 + ii
                            MM(bkPT[:, ii * 128:(ii + 1) * 128], PBD[cur].t[:, i, :], PTBD[cur].t[:, i, :],
                               R=[PBD[cur].b, PTBD[cur].b], W=[bkPT.b])
                    if k <= 3:
                        bkP = nb()
                        for ii in range(4):
                            i = g4 * 4 + ii
                            MM(bkP[:, ii * 128:(ii + 1) * 128], PTBD[cur].t[:, i, :], PBD[cur].t[:, i, :],
                               R=[PTBD[cur].b, PBD[cur].b], W=[bkP.b])
                    Tdst = TBD[par] if k == 5 else TBk[nxt]
                    if k == 0:
                        TT(DVE, Tdst.t[:, o4, :], PBD[cur].t[:, o4, :], bc(ident_b.unsqueeze(1), [128, 4, 128]), ALU.add,
                           R=[PBD[cur].b, cb.b], W=[Tdst.b], relax=True)
                    else:
                        TT(DVE, Tdst.t[:, o4, :].rearrange("p a n -> p (a n)"), bkT_[:, :],
                           TBk[cur].t[:, o4, :].rearrange("p a n -> p (a n)"), ALU.add, R=[bkT_.b, TBk[cur].b],
                           W=[Tdst.b], relax=True)
                    if k <= 4:
                        CP(ACT, PTBD[nxt].t[:, o4, :].rearrange("p a n -> p (a n)"), bkPT[:, :],
                           R=[bkPT.b], W=[PTBD[nxt].b], relax=True)
                    if k <= 3:
                        CP(ACT, PBD[nxt].t[:, o4, :].rearrange("p a n -> p (a n)"), bkP[:, :], R=[bkP.b],
                           W=[PBD[nxt].b], relax=True)
                    yield

        def wkv_part2(c, par, full, Ystore):
            Kt, Bt, Vm = KtmBD[par], BtmBD[par], Vtm[par]
            bkX = [nb(), nb()]
            for i in range(8):
                for h in range(2):
                    ps = slice(h * 64, (h + 1) * 64)
                    MM(bkX[h][ps, i * 64:(i + 1) * 64], AR[ps, i, c, 0, :], STb[ps, i, :], R=[AR.b, STb.b],
                       W=[bkX[h].b], start=True, stop=False)
                for h in range(2):
                    ps = slice(h * 64, (h + 1) * 64)
                    MM(bkX[h][ps, i * 64:(i + 1) * 64], Aak[par].t[ps, i, :], Vm.t[ps, i, :], R=[Aak[par].b, Vm.b],
                       W=[bkX[h].b], start=False, stop=True)
            for h in range(2):
                ps = slice(h * 64, (h + 1) * 64)
                CP(ACT if h == 0 else DVE, X0b.t[ps, :, :].rearrange("p a b -> p (a b)"), bkX[h][ps, :],
                   R=[bkX[h].b], W=[X0b.b])
            yield
            bkX2 = nb()
            for i in range(8):
                MM(bkX2[:, i * 64:(i + 1) * 64], TBD[par].t[:, i, :], X0b.t[:, i, :], R=[TBD[par].b, X0b.b],
                   W=[bkX2.b])
            CP(ACT, Xb.t[:, :, :].rearrange("p a b -> p (a b)"), bkX2[:, :], R=[bkX2.b], W=[Xb.b])
            yield
            bkS = nb()
            for i in range(8):
                MM(bkS[:, i * 64:(i + 1) * 64], Bt.t[:, i, :], Xb.t[:, i, :], R=[Bt.b, Xb.b], W=[bkS.b], start=True,
                   stop=False)
                MM(bkS[:, i * 64:(i + 1) * 64], Kt.t[:, i, :], Vm.t[:, i, :], R=[Kt.b, Vm.b], W=[bkS.b], start=False,
                   stop=True)
            if full:
                bkY = [nb(), nb()]
                for i in range(8):
                    for h in range(2):
                        ps = slice(h * 64, (h + 1) * 64)
                        MM(bkY[h][ps, i * 64:(i + 1) * 64], AR[ps, i, c, 1, :], STb[ps, i, :], R=[AR.b, STb.b],
                           W=[bkY[h].b], start=True, stop=False)
                    for h in range(2):
                        ps = slice(h * 64, (h + 1) * 64)
                        MM(bkY[h][ps, i * 64:(i + 1) * 64], Abr[par].t[ps, i, :], Xb.t[ps, i, :], R=[Abr[par].b, Xb.b],
                           W=[bkY[h].b], start=False, stop=False)
                    for h in range(2):
                        ps = slice(h * 64, (h + 1) * 64)
                        MM(bkY[h][ps, i * 64:(i + 1) * 64], Akr[par].t[ps, i, :], Vm.t[ps, i, :], R=[Akr[par].b, Vm.b],
                           W=[bkY[h].b], start=False, stop=True)
            TT(DVE, Stmp.t[:, :, :], bkS[:, :].rearrange("p (a n) -> p a n", a=8), ST.t[:, :, :], ALU.add,
               R=[bkS.b, ST.b], W=[Stmp.b])
            TT(DVE, ST.t[:, :, :], Stmp.t[:, :, :], bc(gC[:, :, c:c + 1], [128, 8, 64]), ALU.mult, R=[Stmp.b, gC.b],
               W=[ST.b])
            CP(DVE, STb.t[:, :, :], ST.t[:, :, :], R=[ST.b], W=[STb.b])
            if full:
                for h in range(2):
                    ps = slice(h * 64, (h + 1) * 64)
                    CP(ACT, Ystore[ps, 2 * c:2 * c + 2, :], bkY[h][ps, :].rearrange("p (a n) -> p a n", a=2),
                       R=[bkY[h].b], W=[Ystore.b], relax=h > 0)
            yield

        def wkv_all(nch, Vfm, full, Ystore):
            def drain(g):
                for _ in g:
                    pass
            p1 = wkv_part1(0, 0, Vfm, full)
            drain(p1)
            for c in range(nch):
                p2 = wkv_part2(c, c % 2, full, Ystore)
                p1 = wkv_part1(c + 1, (c + 1) % 2, Vfm, full) if c + 1 < nch else iter(())
                done1 = done2 = False
                while not (done1 and done2):
                    if not done2:
                        try:
                            next(p2)
                        except StopIteration:
                            done2 = True
                    for _ in range(5):
                        if done1:
                            break
                        try:
                            next(p1)
                        except StopIteration:
                            done1 = True

        def tm_out(ntok, Ystore, bonus, G):
            nch = ntok // 64
            ng = nch * 8
            Yv = Ystore[:, 0:2 * nch, :].rearrange("p a (b v) -> p (a b) v", v=64)
            ysq = A[1]
            ysv = ysq[:, 0:2 * nch, :].rearrange("p a (b v) -> p (a b) v", v=64)
            ACTV(ysv, Yv, AF.Square, R=[Ystore.b], W=[ysq.b])
            s1, s2, mm, var = (gns[:, k, 0:ng] for k in range(4))
            RED(s1, Yv, ALU.add, R=[Ystore.b], W=[gns.b])
            RED(s2, ysv, ALU.add, R=[ysq.b], W=[gns.b])
            TS(DVE, mm, s1, 1.0 / 64, ALU.mult, R=[gns.b], W=[gns.b])
            TT(DVE, var, mm, mm, ALU.mult, R=[gns.b], W=[gns.b])
            STT(var, s2, 1.0 / 64, var, ALU.mult, ALU.subtract, R=[gns.b], W=[gns.b])
            TS(DVE, var, var, LNX_EPS, ALU.add, R=[gns.b], W=[gns.b])
            ACTV(var, var, AF.Sqrt, R=[gns.b], W=[gns.b])
            RECIP(var, var, R=[gns.b], W=[gns.b])
            TT(DVE, ysv, Yv, bc(mm.unsqueeze(2), [128, ng, 64]), ALU.subtract, R=[Ystore.b, gns.b], W=[ysq.b])
            ynb = Bz[4]
            ynv = ynb[:, 0:2 * nch, :].rearrange("p a (b v) -> p (a b) v", v=64)
            TT(DVE, ynv, ysv, bc(var.unsqueeze(2), [128, ng, 64]), ALU.mult, R=[ysq.b, gns.b], W=[ynb.b])
            ZT = Bz[0]
            for g4 in range(2):
                bkh = [nb(), nb()]
                bvh = [bfv(b_).rearrange("p (a n) -> p a n", a=4) for b_ in bkh]
                for ii in range(4):
                    i = g4 * 4 + ii
                    for c in range(nch):
                        for h in range(2):
                            ps = slice(h * 64, (h + 1) * 64)
                            TR(bvh[h][ps, ii, c * 64:(c + 1) * 64],
                               ynb[ps, 2 * c + i // 4, (i % 4) * 64:(i % 4 + 1) * 64], cb[ps, h * 64:(h + 1) * 64],
                               R=[ynb.b, cb.b], W=[bkh[h].b])
                for ii in range(4):
                    i = g4 * 4 + ii
                    for h in range(2):
                        ps = slice(h * 64, (h + 1) * 64)
                        STT(A[2][ps, i, 0:ntok], bvh[h][ps, ii, 0:ntok], vecs[ps, V_LNW, i:i + 1],
                            bonus[ps, i, 0:ntok], ALU.mult, ALU.add, R=[bkh[h].b, vecs.b, bonus.b], W=[A[2].b],
                            relax=(i > 0 or h > 0))
                    STT(ZT[:, i, 0:ntok], A[2][:, i, 0:ntok], vecs[:, V_LNB, i:i + 1], G[:, i, 0:ntok], ALU.add,
                        ALU.mult, R=[A[2].b, vecs.b, G.b], W=[ZT.b], relax=i > 0)
            for b in range(2):
                slot = wload(f"wo{b}")
                sv = slot.t[:, :].rearrange("p (i c) -> p i c", i=8)
                for o2 in range(2):
                    bk_ = nb()
                    for oo in range(2):
                        oc4 = o2 * 2 + oo
                        for i in range(8):
                            MM(bk_[:, oo * 256:oo * 256 + ntok], sv[:, i, oc4 * 128:(oc4 + 1) * 128], ZT[:, i, 0:ntok],
                               R=[slot.b, ZT.b], W=[bk_.b], start=i == 0, stop=i == 7)
                    oc = 4 * b + o2 * 2
                    TT(DVE, xT[:, oc:oc + 2, 0:ntok], b2(bk_, ntok), xT[:, oc:oc + 2, 0:ntok], ALU.add,
                       R=[bk_.b, xT.b], W=[xT.b])

        def ffn(ntok, l):
            rstd = rows[0]
            rstd_of_x(ntok, Bz[4], rstd)
            hf = Bz[1]
            apply_norm(ntok, rstd, V_NF0 if l == 0 else V_NF1, hf)
            for j in range(11):
                slot = wload(f"f{l}i{j}")
                sv = slot.t[:, :].rearrange("p (k g c) -> p k g c", k=8, g=2)
                for jj in range(2):
                    hc = 2 * j + jj
                    bk_ = nb()
                    for g in range(2):
                        for kc in range(8):
                            MM(bk_[:, g * 256:g * 256 + ntok], sv[:, kc, g, jj * 128:(jj + 1) * 128], hf[:, kc, 0:ntok],
                               R=[slot.b, hf.b], W=[bk_.b], start=kc == 0, stop=kc == 7)
                    sgt = A[2 + hc % 2]
                    ACTV(sgt[:, 0, 0:ntok], bk_[:, 0:ntok], AF.Silu, R=[bk_.b], W=[sgt.b], relax=hc >= 2)
                    TT(DVE, hid_ap[:, hc, 0:ntok], bk_[:, 256:256 + ntok], sgt[:, 0, 0:ntok], ALU.mult,
                       R=[bk_.b, sgt.b], W=hid_bufs, relax=hc > 0)
            for o2 in range(4):
                bk_ = nb()
                for oo in range(2):
                    oc = o2 * 2 + oo
                    slot = wload(f"f{l}o{oc}")
                    sv = slot.t[:, 0:NJ * 128].rearrange("p (j c) -> p j c", j=NJ)
                    for jc in range(NJ):
                        MM(bk_[:, oo * 256:oo * 256 + ntok], sv[:, jc, :], hid_ap[:, jc, 0:ntok],
                           R=[slot.b] + hid_bufs, W=[bk_.b], start=jc == 0, stop=jc == NJ - 1)
                TT(DVE, xT[:, o2 * 2:o2 * 2 + 2, 0:ntok], b2(bk_, ntok), xT[:, o2 * 2:o2 * 2 + 2, 0:ntok], ALU.add,
                   R=[bk_.b] + ([xT.b] if o2 == 0 else []), W=[xT.b], relax=o2 > 0)

        def rope16(src16_tb, src_ap16, nh, ntok, out_write):
            pass

        def shared_kv(ntok, rstd, rope_src):
            nblk = (ntok + 127) // 128
            hkv = Bz[1]
            apply_norm(ntok, rstd, V_KVN, hkv)
            LOAD(ropet[:, :, 0:ntok], rope_src, d_rope, W=[ropet.b])
            slot = wload("wkv")
            sv = slot.t[:, :].rearrange("p (k c) -> p k c", k=8)
            for g2 in range(2):
                bk_ = nb()
                for gg in range(2):
                    g = g2 * 2 + gg
                    for kc in range(8):
                        MM(bk_[0:64, gg * 256:gg * 256 + ntok], sv[:, kc, g * 64:(g + 1) * 64], hkv[:, kc, 0:ntok],
                           R=[slot.b, hkv.b], W=[bk_.b], start=kc == 0, stop=kc == 7)
                    ACTV(Kf[:, g, 0:ntok], bk_[0:64, gg * 256:gg * 256 + ntok], AF.Identity, R=[bk_.b, bk.b],
                         W=[Kf.b], bias=bk[:, g:g + 1])
            for blk in range(nblk):
                tb = min(128, ntok - blk * 128)
                bk_ = nb()
                for kc in range(8):
                    MM(bk_[0:tb, 0:256], hkv[:, kc, blk * 128:blk * 128 + tb], sv[:, kc, 256:512], R=[slot.b, hkv.b],
                       W=[bk_.b], start=kc == 0, stop=kc == 7)
                TT(DVE, Vt[0:tb, 1 + blk, :], bk_[0:tb, 0:256], bvb[0:tb, :], ALU.add, R=[bk_.b, bvb.b], W=[Vt.b])
            CP(DVE, K16[:, :, 0:ntok], Kf[0:16, :, 0:ntok], R=[Kf.b], W=[K16.b])
            for g2 in range(2):
                bk_ = nb()
                for gg in range(2):
                    g = g2 * 2 + gg
                    MM(bk_[0:16, gg * 256:gg * 256 + ntok], p16_b, K16[:, g, 0:ntok], R=[cb.b, K16.b], W=[bk_.b])
                t1 = A[3].t[0:16, 0:2, 0:ntok]
                t2 = A[3].t[0:16, 2:4, 0:ntok]
                cosb = bc(ropet[:, 0, 0:ntok].unsqueeze(1), [16, 2, ntok])
                sinb = bc(ropet[:, 1, 0:ntok].unsqueeze(1), [16, 2, ntok])
                TT(DVE, t1, Kf[0:16, g2 * 2:g2 * 2 + 2, 0:ntok], cosb, ALU.mult, R=[Kf.b, ropet.b], W=[r16.b])
                TT(DVE, t2, bk_[0:16, :].rearrange("p (a n) -> p a n", a=2)[:, :, 0:ntok], sinb, ALU.mult,
                   R=[bk_.b, ropet.b], W=[r16.b])
                TT(DVE, Kf[0:16, g2 * 2:g2 * 2 + 2, 0:ntok], t1, t2, ALU.add, R=[r16.b], W=[Kf.b])
            CP(ACT, KF[:, :, 128:128 + ntok], Kf[:, :, 0:ntok], R=[Kf.b], W=[KF.b])

        def roll_window(ntok):
            CP(POOL, KF[:, :, 0:128], KF[:, :, ntok:ntok + 128], R=[KF.b], W=[KF.b])
            CP(POOL, Vt[:, 0, :], Vt[:, ntok // 128, :], R=[Vt.b], W=[Vt.b])

        def attention(ntok, rstd, mask_first, mask_rest):
            nblk = (ntok + 127) // 128
            hq = Bz[0]
            apply_norm(ntok, rstd, V_NM1, hq)
            Qb = Bz[2:4]
            def qv(h):
                return Qb[h // 8][0:64, h % 8, 0:ntok]
            for b in range(2):
                slot = wload(f"wq{b}")
                sv = slot.t[:, :].rearrange("p (k c) -> p k c", k=8)
                for h2 in range(4):
                    bk_ = nb()
                    for hh in range(2):
                        hl = h2 * 2 + hh
                        for kc in range(8):
                            MM(bk_[0:64, hh * 256:hh * 256 + ntok], sv[:, kc, hl * 64:(hl + 1) * 64], hq[:, kc, 0:ntok],
                               R=[slot.b, hq.b], W=[bk_.b], start=kc == 0, stop=kc == 7)
                        h = b * 8 + hl
                        ACTV(qv(h), bk_[0:64, hh * 256:hh * 256 + ntok], AF.Identity, R=[bk_.b, bq.b],
                             W=[Qb[b].b], bias=bq[:, h:h + 1], relax=hh > 0)
                    h0 = b * 8 + h2 * 2
                    bk2 = nb()
                    for hh in range(2):
                        MM(bk2[0:16, hh * 256:hh * 256 + ntok], p16_b, Qb[b][0:16, (h0 + hh) % 8, 0:ntok],
                           R=[cb.b, Qb[b].b], W=[bk2.b])
                    t1 = A[3].t[0:16, 0:2, 0:ntok]
                    t2 = A[3].t[0:16, 2:4, 0:ntok]
                    cosb = bc(ropet[:, 0, 0:ntok].unsqueeze(1), [16, 2, ntok])
                    sinb = bc(ropet[:, 1, 0:ntok].unsqueeze(1), [16, 2, ntok])
                    qs = Qb[b][0:16, h0 % 8:h0 % 8 + 2, 0:ntok]
                    TT(DVE, t1, qs, cosb, ALU.mult, R=[Qb[b].b, ropet.b], W=[r16.b])
                    TT(DVE, t2, bk2[0:16, :].rearrange("p (a n) -> p a n", a=2)[:, :, 0:ntok], sinb, ALU.mult,
                       R=[bk2.b, ropet.b], W=[r16.b])
                    TT(DVE, qs, t1, t2, ALU.add, R=[r16.b], W=[Qb[b].b])
            OT = Bz[1]
            units = [(qb, g) for qb in range(nblk) for g in range(4)]
            grp = {}

            def front(idx):
                qb, g = units[idx]
                tb = min(128, ntok - qb * 128)
                nk = 128 + tb
                mk = mask_first if qb == 0 else mask_rest
                u = idx % 3
                sm = asm[u]
                E = Eb[u]
                bkS = [nb(), nb()]
                for j in range(4):
                    h = 4 * g + j
                    o_ = bkS[j // 2][0:tb, (j % 2) * 256:(j % 2) * 256 + nk]
                    MM(o_, Qb[h // 8][0:64, h % 8, qb * 128:qb * 128 + tb], KF[0:64, g, qb * 128:qb * 128 + nk],
                       R=[Qb[h // 8].b, KF.b], W=[bkS[j // 2].b], start=True, stop=(mk == 2))
                    if mk != 2:
                        MM(o_, mkb[0:4, 0:tb], mkb[0:4, 128 + mk * 256:128 + mk * 256 + nk], R=[mkb.b],
                           W=[bkS[j // 2].b], start=False, stop=True)
                for jj in range(2):
                    RED(sm[0:tb, 0, 2 * jj:2 * jj + 2], bkS[jj][0:tb, :].rearrange("p (a n) -> p a n", a=2)[:, :, 0:nk],
                        ALU.max, R=[bkS[jj].b], W=[sm.b], relax=jj > 0)
                sk = sinkb[0:tb, 4 * g:4 * g + 4]
                STT(sm[0:tb, 0, :], sm[0:tb, 0, :], ATT_SCALE, sk, ALU.mult, ALU.max, R=[sm.b, sinkb.b], W=[sm.b])
                TS(DVE, sm[0:tb, 1, :], sm[0:tb, 0, :], -1.0, ALU.mult, R=[sm.b], W=[sm.b])
                TT(DVE, sm[0:tb, 2, :], sk, sm[0:tb, 0, :], ALU.subtract, R=[sm.b, sinkb.b], W=[sm.b])
                sa = asa[u]
                for j in range(4):
                    ACTV(E[0:tb, j, 0:nk], bkS[j // 2][0:tb, (j % 2) * 256:(j % 2) * 256 + nk], AF.Exp,
                         R=[bkS[j // 2].b, sm.b], W=[E.b, sa.b], bias=sm[0:tb, 1, j:j + 1], scale=ATT_SCALE,
                         accum=sa[0:tb, 1, j:j + 1], relax=j > 0)
                ACTV(sa[0:tb, 0, :], sm[0:tb, 2, :], AF.Exp, R=[sm.b], W=[sa.b])

            def back_norm(idx):
                qb, g = units[idx]
                tb = min(128, ntok - qb * 128)
                nk = 128 + tb
                u = idx % 3
                sm = asm[u]
                E = Eb[u]
                sa = asa[u]
                TT(DVE, sm[0:tb, 4, :], sa[0:tb, 1, :], sa[0:tb, 0, :], ALU.add, R=[sa.b], W=[sm.b], relax=True)
                RECIP(sm[0:tb, 5, :], sm[0:tb, 4, :], R=[sm.b], W=[sm.b])
                TT(DVE, E[0:tb, :, 0:nk], E[0:tb, :, 0:nk], bc(sm[0:tb, 5, :].unsqueeze(2), [tb, 4, nk]), ALU.mult,
                   R=[E.b, sm.b], W=[E.b])

            def back_tr(idx):
                qb, g = units[idx]
                tb = min(128, ntok - qb * 128)
                E = Eb[idx % 3]
                bkT = nb()
                tv = bfv(bkT).rearrange("p (a b n) -> p a b n", a=4, b=2)
                for j in range(4):
                    TR(tv[0:128, j, 0, 0:tb], E[0:tb, j, 0:128], cb[0:tb, 0:tb], R=[E.b, cb.b], W=[bkT.b])
                    TR(tv[0:tb, j, 1, 0:tb], E[0:tb, j, 128:128 + tb], cb[0:tb, 0:tb], R=[E.b, cb.b], W=[bkT.b])
                pT = pTb[idx % 2]
                CP(ACT, pT[0:128, :, 0, 0:tb], tv[0:128, :, 0, 0:tb], R=[bkT.b], W=[pT.b])
                CP(ACT, pT[0:tb, :, 1, 0:tb], tv[0:tb, :, 1, 0:tb], R=[bkT.b], W=[pT.b], relax=True)

            def back_pv(idx):
                qb, g = units[idx]
                tb = min(128, ntok - qb * 128)
                pT = pTb[idx % 2]
                if g % 2 == 0:
                    grp["bkO"] = nb()
                    held.append(grp["bkO"])
                bkO = grp["bkO"]
                for j in range(4):
                    hh = j % 2
                    slot = (2 * g + j // 2) % 4
                    ps = slice(hh * 64, (hh + 1) * 64)
                    MM(bkO[ps, slot * 128:slot * 128 + tb], Vt[0:128, qb, g * 64:(g + 1) * 64], pT[0:128, j, 0, 0:tb],
                       R=[Vt.b, pT.b], W=[bkO.b], start=True, stop=False)
                    MM(bkO[ps, slot * 128:slot * 128 + tb], Vt[0:tb, qb + 1, g * 64:(g + 1) * 64],
                       pT[0:tb, j, 1, 0:tb], R=[Vt.b, pT.b], W=[bkO.b], start=False, stop=True)
                if g % 2 == 1:
                    p4 = g // 2
                    CP(ACT, OT[:, p4 * 4:p4 * 4 + 4, qb * 128:qb * 128 + tb],
                       bkO[:, :].rearrange("p (a n) -> p a n", a=4)[:, :, 0:tb], R=[bkO.b], W=[OT.b])
                    held.remove(bkO)

            nu = len(units)
            for it in range(nu + 2):
                if 1 <= it <= nu:
                    back_norm(it - 1)
                if it < nu:
                    front(it)
                if 1 <= it <= nu:
                    back_tr(it - 1)
                if it >= 2:
                    back_pv(it - 2)
            dump(f"OT{ntok}_{bank_state['i']}", OT, [128, 8, NT], BF16)
            for b in range(2):
                slot = wload(f"wp{b}")
                sv = slot.t[:, :].rearrange("p (k c) -> p k c", k=8)
                for o2 in range(2):
                    bk_ = nb()
                    for oo in range(2):
                        oc4 = o2 * 2 + oo
                        for kc in range(8):
                            MM(bk_[:, oo * 256:oo * 256 + ntok], sv[:, kc, oc4 * 128:(oc4 + 1) * 128],
                               OT[:, kc, 0:ntok], R=[slot.b, OT.b], W=[bk_.b], start=kc == 0, stop=kc == 7)
                    for oo in range(2):
                        oc = 4 * b + o2 * 2 + oo
                        STT(xT[:, oc, 0:ntok], bk_[:, oo * 256:oo * 256 + ntok], vecs[:, V_BO, oc:oc + 1],
                            xT[:, oc, 0:ntok], ALU.add, ALU.add, R=[bk_.b, vecs.b, xT.b], W=[xT.b])

        def final_out(ntok, ydst):
            nblk = (ntok + 127) // 128
            rstd = rows[0]
            rstd_of_x(ntok, Bz[4], rstd)
            yf = A[0]
            apply_norm(ntok, rstd, V_NFIN, yf)
            yo = A[1:3]
            for blk in range(nblk):
                tb = min(128, ntok - blk * 128)
                yov = yo[blk].t[:, :, :].rearrange("p a n -> p (a n)")
                for k4 in range(2):
                    bk_ = nb()
                    for kk_ in range(4):
                        kc = k4 * 4 + kk_
                        TR(bk_[0:tb, kk_ * 128:(kk_ + 1) * 128], yf[:, kc, blk * 128:blk * 128 + tb], ident_f,
                           R=[yf.b, cst.b], W=[bk_.b])
                    CP(ACT if k4 == 0 else DVE, yov[0:tb, k4 * 512:(k4 + 1) * 512], bk_[0:tb, :], R=[bk_.b],
                       W=[yo[blk].b])
                STORE(ydst[blk * 128:blk * 128 + tb, :], yov[0:tb, 0:1024], R=[yo[blk].b], ds=d_outb[blk])

        def out_state(st_dst, sh_dst):
            STORE(st_dst, ST.t[:, :, :], R=[ST.b], ds=d_st)
            STORE(sh_dst, hprev.t[:, :], R=[hprev.b], ds=d_st)

        def out_cache(ck_dst, cv_dst, ntok):
            nk = min(ntok, 128)
            bk_ = nb()
            tv = bfv(bk_)
            for g in range(4):
                TR(tv[0:nk, g * 64:(g + 1) * 64], KF[0:64, g, 128 + ntok - nk:128 + ntok], cb[0:64, 0:64],
                   R=[KF.b, cb.b], W=[bk_.b])
            CP(DVE, kvo[0:nk, 0, :], tv[0:nk, 0:256], R=[bk_.b], W=[kvo.b])
            lastblk = (ntok + 127) // 128
            CP(DVE, kvo[0:nk, 1, :], Vt[0:nk, lastblk, :], R=[Vt.b], W=[kvo.b])
            STORE(ck_dst[128 - nk:128, :], kvo[0:nk, 0, :], R=[kvo.b], ds=d_st)
            STORE(cv_dst[128 - nk:128, :], kvo[0:nk, 1, :], R=[kvo.b], ds=d_st)

        stg = {"n": 0}

        def stage():
            PHASES.append(P_pe.n)
            stg["n"] += 1
            if _STOP is not None and stg["n"] >= _STOP:
                raise StopBuild()

        def run_tile(xsrc, ntok, mode, rope_src=None, ydst=None, mask_first=0, mask_rest=0):
            nch = ntok // 64
            full = mode != "warm"
            stage()
            load_x(xsrc, ntok)
            flush_stores()
            stage()
            bonus, G, Vfm = time_mix(ntok, mode)
            stage()
            Ystore = A[4]
            wkv_all(nch, Vfm, full, Ystore)
            stage()
            if not full:
                return
            tm_out(ntok, Ystore, bonus, G)
            stage()
            ffn(ntok, 0)
            stage()
            rstd = rows[0]
            rstd_of_x(ntok, Bz[4], rstd)
            shared_kv(ntok, rstd, rope_src)
            stage()
            if mode == "l0":
                roll_window(ntok)
                return
            attention(ntok, rstd, mask_first, mask_rest)
            stage()
            ffn(ntok, 1)
            stage()
            final_out(ntok, ydst)
            stage()

        try:
          for t in range(n_warm):
              last = t == n_warm - 1
              if n_warm > 1:
                  pc_emit(len(pc_queue) if t >= n_warm - 2 else pc_per_tile)
              run_tile(xw[t * NT:(t + 1) * NT, :], NT, "l0" if last else "warm", rope_src=ropew)
          for t in range(n_main):
              run_tile(xm[t * NT:(t + 1) * NT, :], NT, "full", rope_src=ropem[:, :, t * NT:(t + 1) * NT],
                       ydst=y_m[t * NT:(t + 1) * NT, :], mask_first=1 if t == 0 else 0, mask_rest=0)
              if t < n_main - 1:
                  roll_window(NT)
          out_state(st_m, sh_m)
          out_cache(ck_m, cv_m, NT)
          flush_stores()
          LOAD(ST.t[:, :, :], st0, d_const, W=[ST.b])
          LOAD(hprev.t[:, :], sh0, d_const, W=[hprev.b])
          LOAD(kvo[:, 0, :], ck, d_const, W=[kvo.b])
          LOAD(kvo[:, 1, :], cv, d_const, W=[kvo.b])
          for dst in (ST, hprev, kvo):
              dst.b.w = (d_const, d_const.n)
          CP(ACT, STb.t[:, :, :], ST.t[:, :, :], R=[ST.b], W=[STb.b])
          CP(DVE, Vt[:, 0, :], kvo[:, 1, :], R=[kvo.b], W=[Vt.b])
          bk_ = nb()
          for g in range(4):
              TR(bk_[0:64, g * 128:(g + 1) * 128], kvo[:, 0, g * 64:(g + 1) * 64], ident_f, R=[kvo.b, cst.b], W=[bk_.b])
          CP(ACT, KF[:, :, 0:128], bk_[0:64, :].rearrange("p (a n) -> p a n", a=4), R=[bk_.b], W=[KF.b])
          S.dma(SP, lambda e: e.dma_start(out=ck_s[0:64, :], in_=ck[64:128, :]), d_st, R=(), W=())
          S.dma(SP, lambda e: e.dma_start(out=cv_s[0:64, :], in_=cv[64:128, :]), d_st, R=(), W=())
          run_tile(xs, 64, "full", rope_src=ropes, ydst=y_s, mask_first=2, mask_rest=2)
          out_state(st_s, sh_s)
          out_cache(ck_s, cv_s, 64)
          flush_stores()
        except StopBuild:
            flush_stores()
        for dsm in (d_out, d_st, d_outb[0], d_outb[1]):
            if dsm.n:
                SP.ops.append({"k": "w", "p": dsm, "v": dsm.n})

        Sched.finalize(prods)
        with nc.Block() as block:
            @block.tensor
            def _(e):
                Sched.replay(e, PE)

            @block.scalar
            def _(e):
                Sched.replay(e, ACT)

            @block.vector
            def _(e):
                Sched.replay(e, DVE)

            @block.gpsimd
            def _(e):
                Sched.replay(e, POOL)

            @block.sync
            def _(e):
                Sched.replay(e, SP)
    global LAST_NOPS
    LAST_NOPS = {E.name: len(E.ops) for E in (PE, ACT, DVE, POOL, SP)}
    return nc


def _nat(v):
    return np.ascontiguousarray(np.asarray(v, np.float32).reshape(8, 128).T)


def _pf(v):
    return np.ascontiguousarray(np.asarray(v, np.float32).reshape(2, 8, 64).transpose(0, 2, 1).reshape(128, 8))


def _consts():
    c = np.zeros((128, CS_W), np.float32)
    p = np.arange(128)
    c[:, CS_ID:CS_ID + 128] = np.eye(128)
    c[:, CS_BLK:CS_BLK + 128] = (p[:, None] // 64 == p[None, :] // 64)
    j = (p % 64)[:, None]
    t = np.arange(64)[None, :]
    ms = (j < t).astype(np.float32)
    mi = (j <= t).astype(np.float32)
    mst = (t < j).astype(np.float32)
    i64 = (j == t).astype(np.float32)
    c[:, CS_MS:CS_MS + 64] = ms
    c[:, CS_MI:CS_MI + 64] = mi
    c[:, CS_MST:CS_MST + 64] = mst
    c[:, CS_I64:CS_I64 + 64] = i64
    sel = np.zeros((128, 2, 64), np.float32)
    sel[:64, 0] = 1
    sel[64:, 1] = 1
    c[:, CS_MSBD:CS_MSBD + 128] = (sel * ms[:, None, :]).reshape(128, 128)
    c[:, CS_MIBD:CS_MIBD + 128] = (sel * mi[:, None, :]).reshape(128, 128)
    c[:, CS_MSTBD:CS_MSTBD + 128] = (sel * mst[:, None, :]).reshape(128, 128)
    c[:, CS_SELBD:CS_SELBD + 128] = sel.reshape(128, 128)
    for i in range(8):
        c[i + 8, CS_P16 + i] = -1.0
        c[i, CS_P16 + i + 8] = 1.0
    c[:, CS_RESET:CS_RESET + NT] = (np.arange(NT) % 64 != 0)[None, :]
    return c


def _rope_tab(pos):
    half = 8
    inv = np.power(np.float32(500000.0), -np.arange(half, dtype=np.float32) * np.float32(2.0 / 16)).astype(np.float32)
    ang = pos.astype(np.float32)[:, None] * inv[None, :]
    cos = np.cos(ang).astype(np.float32).T
    sin = np.sin(ang).astype(np.float32).T
    out = np.zeros((16, 2, len(pos)), np.float32)
    out[0:8, 0] = cos
    out[8:16, 0] = cos
    out[0:8, 1] = sin
    out[8:16, 1] = sin
    return out


_NC_CACHE = {}
LAST_NOPS = None
_HOOK = None
PHASES = []
_DBG = False
_STOP = None


class StopBuild(Exception):
    pass


def _run(inputs, n_warm, n_main, seq_half, past_len, dbg=False):
    f = lambda k: np.asarray(inputs[k], np.float32)
    x_prompt, x_sample = f("x_prompt"), f("x_sample")
    nb_p = x_prompt.shape[0]
    ncores = 2 * nb_p
    dbg = dbg or _DBG
    key = (n_warm, n_main, dbg)
    if key not in _NC_CACHE:
        _NC_CACHE[key] = build(n_warm, n_main, dbg)
    nc = _NC_CACHE[key]
    NW, NM = n_warm * NT, n_main * NT
    assert NM == seq_half and NW == seq_half
    vec_list = [None] * NVEC
    vec_list[V_NM0] = _nat(f("norm_mix")[0]); vec_list[V_NF0] = _nat(f("norm_ffn")[0])
    vec_list[V_NM1] = _nat(f("norm_mix")[1]); vec_list[V_NF1] = _nat(f("norm_ffn")[1])
    vec_list[V_KVN] = _nat(f("kv_norm")); vec_list[V_NFIN] = _nat(f("norm_final"))
    for j in range(6):
        vec_list[V_MU0 + j] = _nat(f("rw_mu")[0, j])
    vec_list[V_BO] = _nat(f("b_o")[0])
    vec_list[V_W0] = _pf(f("rw_w0")[0]); vec_list[V_A0] = _pf(f("rw_a0")[0])
    vec_list[V_KK] = _pf(f("rw_k_k")[0]); vec_list[V_KA] = _pf(f("rw_k_a")[0])
    vec_list[V_C1] = None
    vec_list[V_RK] = _pf(f("rw_r_k")[0].reshape(-1))
    vec_list[V_LNW] = _pf(f("rw_lnx_w")[0]); vec_list[V_LNB] = _pf(f("rw_lnx_b")[0])
    vec_list[V_C1] = np.zeros((128, 8), np.float32)
    vecs = np.ascontiguousarray(np.stack(vec_list, axis=1))
    cst = _consts()
    bq = np.ascontiguousarray(f("b_q")[0].reshape(16, 64).T)
    bkk = np.ascontiguousarray(f("b_kv")[0:256].reshape(4, 64).T)
    bvb = np.ascontiguousarray(np.broadcast_to(f("b_kv")[256:512][None, :], (128, 256)))
    sinkb = np.ascontiguousarray(np.broadcast_to(f("attn_sinks")[0].reshape(1, 16), (128, 16)))
    masku = np.zeros((4, 128), np.float32)
    masku[0, 0:64] = 1.0
    masku[1, 64:128] = 1.0
    masku[2, :] = 1.0
    shared = {
        "vecs": vecs, "cst": cst, "bq": bq, "bk": bkk, "bvb": bvb, "sinkb": sinkb,
        "rw_w_rkv": np.ascontiguousarray(f("rw_w_rkv")[0]), "rw_w1": np.ascontiguousarray(f("rw_w1")[0]),
        "rw_w2": np.ascontiguousarray(f("rw_w2")[0]), "rw_a1": np.ascontiguousarray(f("rw_a1")[0]),
        "rw_a2": np.ascontiguousarray(f("rw_a2")[0]), "rw_g1": np.ascontiguousarray(f("rw_g1")[0]),
        "rw_g2": np.ascontiguousarray(f("rw_g2")[0]), "rw_w_o": np.ascontiguousarray(f("rw_w_o")[0]),
        "w_kv": f("w_kv"), "w_q": np.ascontiguousarray(f("w_q")[0]), "w_o": np.ascontiguousarray(f("w_o")[0]),
        "ffn_w_in": f("ffn_w_in"), "ffn_w_out": f("ffn_w_out"),
        "ropes": _rope_tab(past_len + np.arange(64)),
    }
    in_maps = []
    state_wkv, state_shift = f("state_wkv")[0], f("state_shift")[0]
    cache_k, cache_v = f("cache_k"), f("cache_v")
    for c in range(ncores):
        b, half = c // 2, c % 2
        m = dict(shared)
        if half == 0:
            m["xw"] = np.zeros((NW, D), np.float32)
            m["xm"] = np.ascontiguousarray(x_prompt[b, 0:NM])
            pos_m = np.arange(NM)
            pos_w = np.arange(NT)
            first_masked = True
        else:
            m["xw"] = np.ascontiguousarray(x_prompt[b, 0:NW])
            m["xm"] = np.ascontiguousarray(x_prompt[b, NW:NW + NM])
            pos_m = NW + np.arange(NM)
            pos_w = NW - NT + np.arange(NT)
            first_masked = False
        mv = np.zeros((4, 2, 256), np.float32)
        mv[0, :, 192:256] = NEG
        mv[1, :, 0:64] = NEG
        if first_masked:
            mv[2, 1, 0:128] = NEG
        m["masku"] = masku
        m["maskv"] = mv
        m["ropem"] = _rope_tab(pos_m)
        m["ropew"] = _rope_tab(pos_w)
        sb = c % x_sample.shape[0]
        m["xs"] = np.ascontiguousarray(x_sample[sb])
        s = state_wkv[sb]
        m["st0"] = np.ascontiguousarray(s.reshape(2, 8, 64, 64).transpose(0, 3, 1, 2).reshape(128, 8, 64))
        m["sh0"] = _nat(state_shift[sb])
        m["ck"] = np.ascontiguousarray(cache_k[sb].reshape(128, 256))
        m["cv"] = np.ascontiguousarray(cache_v[sb].reshape(128, 256))
        in_maps.append(m)
    if _HOOK is not None:
        R = _HOOK(nc, in_maps)
    else:
        res = run_bass_kernel_spmd(nc, in_maps, core_ids=list(range(ncores)))
        R = res.results

    def unst(a):
        return np.ascontiguousarray(np.asarray(a).reshape(2, 64, 8, 64).transpose(0, 2, 3, 1).reshape(16, 64, 64))

    def unsh(a):
        return np.ascontiguousarray(np.asarray(a).T.reshape(-1))

    y_prompt = np.stack([np.concatenate([R[2 * b]["y_m"], R[2 * b + 1]["y_m"]], axis=0) for b in range(nb_p)])
    nsb = x_sample.shape[0]
    y_sample = np.stack([R[b]["y_s"] for b in range(nsb)])
    wkv_p = np.stack([unst(R[2 * b + 1]["st_m"]) for b in range(nb_p)])[None]
    shift_p = np.stack([unsh(R[2 * b + 1]["sh_m"]) for b in range(nb_p)])[None]
    k_p = np.stack([np.asarray(R[2 * b + 1]["ck_m"]).reshape(128, 4, 64) for b in range(nb_p)])
    v_p = np.stack([np.asarray(R[2 * b + 1]["cv_m"]).reshape(128, 4, 64) for b in range(nb_p)])
    wkv_s = np.stack([unst(R[b]["st_s"]) for b in range(nsb)])[None]
    shift_s = np.stack([unsh(R[b]["sh_s"]) for b in range(nsb)])[None]
    k_s = np.stack([np.asarray(R[b]["ck_s"]).reshape(128, 4, 64) for b in range(nsb)])
    v_s = np.stack([np.asarray(R[b]["cv_s"]).reshape(128, 4, 64) for b in range(nsb)])
    outs = (y_prompt, y_sample, wkv_p, shift_p, k_p, v_p, wkv_s, shift_s, k_s, v_s)
    outs = tuple(np.ascontiguousarray(o, dtype=np.float32) for o in outs)
    return outs


def kernel(**inputs):
    seq = np.asarray(inputs["x_prompt"]).shape[1]
    half = seq // 2
    return _run(inputs, half // NT, half // NT, half, 1024)
```
